# Optimizing a Trainium2 kernel written in Bass

```python
import math
import jax, jax.numpy as jnp
from jax import lax
import numpy as np

D_MODEL = 2048
BATCH = 4
SEQ = 4096
DEPTH = 1

HEAD_DIM = 128
N_HEADS = D_MODEL // HEAD_DIM
N_HEADS_A = N_HEADS // 2
N_HEADS_B = N_HEADS - N_HEADS_A
WIDTH_A = N_HEADS_A * HEAD_DIM
WIDTH_B = N_HEADS_B * HEAD_DIM

DIL_CONFIGS = ((128, 1), (512, 4), (2048, 16))
DIL_BLOCK = 128

NSA_KV_GROUPS = 2
NSA_HPG = N_HEADS_B // NSA_KV_GROUPS
KV_WIDTH_B = NSA_KV_GROUPS * HEAD_DIM
CMP_BLOCK = 32
CMP_STRIDE = 16
CMP_HIDDEN = 2 * HEAD_DIM
SEL_BLOCK = 64
SEL_TOP_N = 16
SEL_FORCED_LOCAL = 2
SEL_Q_CHUNK = 64
FORCE_BONUS = 1.0e3
WIN_SIZE = 512
WIN_BLOCK = 128
N_GATES = 3

FFN_HIDDEN = -(-(8 * D_MODEL) // (3 * 256)) * 256
RMS_EPS = 1e-6
NEG_INF = -1e30

COL_SIZES = (WIDTH_A, WIDTH_A, WIDTH_A,
             WIDTH_B,
             KV_WIDTH_B, KV_WIDTH_B,
             KV_WIDTH_B, KV_WIDTH_B,
             KV_WIDTH_B, KV_WIDTH_B,
             N_HEADS_B * N_GATES)
P_IN = sum(COL_SIZES)

kernel_name = "hybrid_dilated_nsa_swiglu_block"


def rmsnorm(x, g):
    xf = x.astype(jnp.float32)
    y = xf * lax.rsqrt(jnp.mean(xf * xf, axis=-1, keepdims=True) + RMS_EPS)
    return (y * g.astype(jnp.float32)).astype(x.dtype)


def alibi_slopes(n):
    return jnp.exp2(-8.0 * jnp.arange(1, n + 1, dtype=jnp.float32) / n)


def masked_softmax_f32(s, mask):
    s = jnp.where(mask, s, NEG_INF)
    m = jnp.max(s, axis=-1, keepdims=True)
    e = jnp.where(mask, jnp.exp(s - m), 0.0)
    return e / jnp.maximum(jnp.sum(e, axis=-1, keepdims=True), 1e-30)


def split_heads(a, n):
    b, s, _ = a.shape
    return a.reshape(b, s, n, HEAD_DIM).transpose(0, 2, 1, 3)


def dilated_attention_config(q, k, v, slopes, window, dilation):
    B, H, S, hd = q.shape
    L = S // dilation
    nb = -(-L // DIL_BLOCK)
    Lp = nb * DIL_BLOCK
    W = window // dilation
    n_prev = -(-W // DIL_BLOCK)
    scale = 1.0 / math.sqrt(hd)

    def to_sub(a):
        a = a.reshape(B, H, L, dilation, hd).transpose(0, 1, 3, 2, 4)
        a = jnp.pad(a, ((0, 0), (0, 0), (0, 0), (0, Lp - L), (0, 0)))
        return a.reshape(B, H, dilation, nb, DIL_BLOCK, hd)

    def band(a):
        a = jnp.pad(a, ((0, 0), (0, 0), (0, 0), (n_prev, 0), (0, 0), (0, 0)))
        return jnp.concatenate([a[:, :, :, i:i + nb] for i in range(n_prev + 1)], axis=4)

    qs = to_sub(q)
    kb, vb = band(to_sub(k)), band(to_sub(v))
    qi = jnp.arange(nb)[:, None] * DIL_BLOCK + jnp.arange(DIL_BLOCK)[None, :]
    kj = (jnp.arange(nb)[:, None] - n_prev) * DIL_BLOCK + jnp.arange((n_prev + 1) * DIL_BLOCK)[None, :]
    rel = qi[:, :, None] - kj[:, None, :]
    mask = (rel >= 0) & (rel <= W) & (kj[:, None, :] >= 0)
    s = jnp.einsum('bhrnqd,bhrnkd->bhrnqk', qs, kb).astype(jnp.float32) * scale
    s = s - slopes[:, None, None, None, None] * (rel * dilation).astype(jnp.float32)
    s = jnp.where(mask, s, NEG_INF)
    lse = jax.nn.logsumexp(s, axis=-1)
    p = jnp.exp(s - lse[..., None])
    o = jnp.einsum('bhrnqk,bhrnkd->bhrnqd', p.astype(v.dtype), vb)
    o = o.reshape(B, H, dilation, Lp, hd)[:, :, :, :L].transpose(0, 1, 3, 2, 4).reshape(B, H, S, hd)
    lse = lse.reshape(B, H, dilation, Lp)[:, :, :, :L].transpose(0, 1, 3, 2).reshape(B, H, S)
    return o, lse


def dilated_mixture(q, k, v, slopes):
    outs, lses = [], []
    for window, dilation in DIL_CONFIGS:
        o, l = dilated_attention_config(q, k, v, slopes, window, dilation)
        outs.append(o)
        lses.append(l)
    w = jax.nn.softmax(jnp.stack(lses, axis=0), axis=0)
    out = sum(w[i][..., None].astype(q.dtype) * outs[i] for i in range(len(outs)))
    return out


def compress_blocks(kv, pe, w1, w2):
    B, G, S, hd = kv.shape
    nc = S // CMP_STRIDE
    r = CMP_BLOCK // CMP_STRIDE
    n_cmp = nc - r + 1
    chunks = kv.reshape(B, G, nc, CMP_STRIDE, hd)
    blocks = jnp.concatenate([chunks[:, :, i:i + n_cmp] for i in range(r)], axis=3)
    blocks = blocks + pe
    flat = blocks.reshape(B, G, n_cmp, CMP_BLOCK * hd)
    return jax.nn.silu(flat @ w1) @ w2


def nsa_compressed(q, kc, vc, slopes):
    S = q.shape[3]
    n_cmp = kc.shape[2]
    scale = 1.0 / math.sqrt(q.shape[-1])
    t = jnp.arange(S)
    end = jnp.arange(n_cmp) * CMP_STRIDE + CMP_BLOCK - 1
    rel = t[:, None] - end[None, :]
    s = jnp.einsum('bghtd,bgcd->bghtc', q, kc).astype(jnp.float32) * scale
    s = s - slopes[:, :, None, None] * rel.astype(jnp.float32)
    p = masked_softmax_f32(s, rel >= 0)
    o = jnp.einsum('bghtc,bgcd->bghtd', p.astype(vc.dtype), vc)
    return o, p


def nsa_selected(q, k, v, p_cmp, slopes):
    B, G, Hq, S, hd = q.shape
    n_cmp = p_cmp.shape[-1]
    n_sel = S // SEL_BLOCK
    top_n = min(SEL_TOP_N, n_sel)
    scale = 1.0 / math.sqrt(hd)
    cstart = jnp.arange(n_cmp)[:, None] * CMP_STRIDE
    sstart = jnp.arange(n_sel)[None, :] * SEL_BLOCK
    overlap = ((cstart < sstart + SEL_BLOCK) & (cstart + CMP_BLOCK > sstart)).astype(jnp.float32)
    imp = jnp.einsum('bghtc,cn->bgtn', p_cmp, overlap)
    t = jnp.arange(S)
    blk = jnp.arange(n_sel)[None, :]
    cur = (t // SEL_BLOCK)[:, None]
    valid = blk <= cur
    forced = (blk == 0) | ((blk <= cur) & (blk > cur - SEL_FORCED_LOCAL))
    score = jnp.where(valid, imp + jnp.where(forced, FORCE_BONUS, 0.0), -1.0)
    top_s, top_i = lax.top_k(score, top_n)
    sel_ok = top_s >= 0.0

    nq = S // SEL_Q_CHUNK
    qc = q.reshape(B, G, Hq, nq, SEL_Q_CHUNK, hd).transpose(3, 0, 1, 2, 4, 5)
    ic = top_i.reshape(B, G, nq, SEL_Q_CHUNK, top_n).transpose(2, 0, 1, 3, 4)
    oc = sel_ok.reshape(B, G, nq, SEL_Q_CHUNK, top_n).transpose(2, 0, 1, 3, 4)
    tc = t.reshape(nq, SEL_Q_CHUNK)
    kblocks = k.reshape(B, G, n_sel, SEL_BLOCK, hd)
    vblocks = v.reshape(B, G, n_sel, SEL_BLOCK, hd)
    bi = jnp.arange(B)[:, None, None, None]
    gi = jnp.arange(G)[None, :, None, None]
    offs = jnp.arange(SEL_BLOCK)

    def one_chunk(args):
        q_, i_, ok_, t_ = args
        kg = kblocks[bi, gi, i_]
        vg = vblocks[bi, gi, i_]
        pos = i_[..., None] * SEL_BLOCK + offs
        rel = t_[None, None, :, None, None] - pos
        mask = ok_[..., None] & (rel >= 0)
        s = jnp.einsum('bghqd,bgqnkd->bghqnk', q_, kg).astype(jnp.float32) * scale
        s = s - slopes[:, :, None, None, None] * rel[:, :, None].astype(jnp.float32)
        s = jnp.where(mask[:, :, None], s, NEG_INF)
        s = s.reshape(B, G, Hq, SEL_Q_CHUNK, top_n * SEL_BLOCK)
        p = jax.nn.softmax(s, axis=-1)
        return jnp.einsum('bghqm,bgqmd->bghqd', p.astype(v.dtype),
                          vg.reshape(B, G, SEL_Q_CHUNK, top_n * SEL_BLOCK, hd))

    o = lax.map(one_chunk, (qc, ic, oc, tc))
    return o.transpose(1, 2, 3, 0, 4, 5).reshape(B, G, Hq, S, hd)


def nsa_window(q, k, v, slopes):
    B, G, Hq, S, hd = q.shape
    nb = S // WIN_BLOCK
    n_prev = -(-(WIN_SIZE - 1) // WIN_BLOCK)
    scale = 1.0 / math.sqrt(hd)
    qb = q.reshape(B, G, Hq, nb, WIN_BLOCK, hd)

    def band(a):
        a = a.reshape(B, G, nb, WIN_BLOCK, hd)
        a = jnp.pad(a, ((0, 0), (0, 0), (n_prev, 0), (0, 0), (0, 0)))
        return jnp.concatenate([a[:, :, i:i + nb] for i in range(n_prev + 1)], axis=3)

    kb, vb = band(k), band(v)
    qi = jnp.arange(nb)[:, None] * WIN_BLOCK + jnp.arange(WIN_BLOCK)[None, :]
    kj = (jnp.arange(nb)[:, None] - n_prev) * WIN_BLOCK + jnp.arange((n_prev + 1) * WIN_BLOCK)[None, :]
    rel = qi[:, :, None] - kj[:, None, :]
    mask = (rel >= 0) & (rel < WIN_SIZE) & (kj[:, None, :] >= 0)
    s = jnp.einsum('bghnqd,bgnkd->bghnqk', qb, kb).astype(jnp.float32) * scale
    s = s - slopes[:, :, None, None, None] * rel.astype(jnp.float32)
    p = masked_softmax_f32(s, mask)
    o = jnp.einsum('bghnqk,bgnkd->bghnqd', p.astype(v.dtype), vb)
    return o.reshape(B, G, Hq, S, hd)


def setup_inputs(seed: int = 0) -> dict:
    key = jax.random.key(seed)
    ks = jax.random.split(key, 20)
    f32 = jnp.float32
    nrm = lambda k, shape, fan_in: jax.random.normal(k, shape, f32) * (fan_in ** -0.5)
    gain = lambda k, shape: 1.0 + 0.02 * jax.random.normal(k, shape, f32)
    return {
        "x": jax.random.normal(ks[0], (BATCH, SEQ, D_MODEL), f32),
        "norm1_g": gain(ks[1], (DEPTH, D_MODEL)),
        "w_in": nrm(ks[2], (DEPTH, D_MODEL, P_IN), D_MODEL),
        "cmp_pe_k": 0.1 * jax.random.normal(ks[3], (DEPTH, CMP_BLOCK, HEAD_DIM), f32),
        "cmp_w1_k": nrm(ks[4], (DEPTH, CMP_BLOCK * HEAD_DIM, CMP_HIDDEN), CMP_BLOCK * HEAD_DIM),
        "cmp_w2_k": nrm(ks[5], (DEPTH, CMP_HIDDEN, HEAD_DIM), CMP_HIDDEN),
        "cmp_pe_v": 0.1 * jax.random.normal(ks[6], (DEPTH, CMP_BLOCK, HEAD_DIM), f32),
        "cmp_w1_v": nrm(ks[7], (DEPTH, CMP_BLOCK * HEAD_DIM, CMP_HIDDEN), CMP_BLOCK * HEAD_DIM),
        "cmp_w2_v": nrm(ks[8], (DEPTH, CMP_HIDDEN, HEAD_DIM), CMP_HIDDEN),
        "grp_norm_a": gain(ks[9], (DEPTH, WIDTH_A)),
        "grp_norm_b": gain(ks[10], (DEPTH, WIDTH_B)),
        "w_out": nrm(ks[11], (DEPTH, WIDTH_A + WIDTH_B, D_MODEL), WIDTH_A + WIDTH_B),
        "norm2_g": gain(ks[12], (DEPTH, D_MODEL)),
        "w_gate": nrm(ks[13], (DEPTH, D_MODEL, FFN_HIDDEN), D_MODEL),
        "w_up": nrm(ks[14], (DEPTH, D_MODEL, FFN_HIDDEN), D_MODEL),
        "w_down": nrm(ks[15], (DEPTH, FFN_HIDDEN, D_MODEL), FFN_HIDDEN),
        "final_g": gain(ks[16], (D_MODEL,)),
    }


def reference(x, norm1_g, w_in, cmp_pe_k, cmp_w1_k, cmp_w2_k, cmp_pe_v, cmp_w1_v, cmp_w2_v,
              grp_norm_a, grp_norm_b, w_out, norm2_g, w_gate, w_up, w_down, final_g):
    B, S, _ = x.shape
    G, Hq = NSA_KV_GROUPS, NSA_HPG
    slopes = alibi_slopes(N_HEADS)
    slopes_a = slopes[0::2]
    slopes_b = slopes[1::2].reshape(G, Hq)
    offsets = np.cumsum(COL_SIZES)[:-1].tolist()
    h = x
    for l in range(DEPTH):
        hn = rmsnorm(h, norm1_g[l])
        proj = hn @ w_in[l]
        (q_a, k_a, v_a, q_b, k_c, v_c, k_s, v_s, k_w, v_w, g_b) = jnp.split(proj, offsets, axis=-1)

        o_a = dilated_mixture(split_heads(q_a, N_HEADS_A), split_heads(k_a, N_HEADS_A),
                              split_heads(v_a, N_HEADS_A), slopes_a)
        o_a = o_a.transpose(0, 2, 1, 3).reshape(B, S, WIDTH_A)

        qb = split_heads(q_b, N_HEADS_B).reshape(B, G, Hq, S, HEAD_DIM)
        kc_raw, vc_raw = split_heads(k_c, G), split_heads(v_c, G)
        kc = compress_blocks(kc_raw, cmp_pe_k[l], cmp_w1_k[l], cmp_w2_k[l])
        vc = compress_blocks(vc_raw, cmp_pe_v[l], cmp_w1_v[l], cmp_w2_v[l])
        o_cmp, p_cmp = nsa_compressed(qb, kc, vc, slopes_b)
        o_sel = nsa_selected(qb, split_heads(k_s, G), split_heads(v_s, G), p_cmp, slopes_b)
        o_win = nsa_window(qb, split_heads(k_w, G), split_heads(v_w, G), slopes_b)
        gates = jax.nn.sigmoid(g_b.reshape(B, S, N_HEADS_B, N_GATES)).transpose(0, 2, 1, 3)
        gates = gates.reshape(B, G, Hq, S, N_GATES)
        o_b = (gates[..., 0:1] * o_cmp + gates[..., 1:2] * o_sel + gates[..., 2:3] * o_win)
        o_b = o_b.reshape(B, N_HEADS_B, S, HEAD_DIM).transpose(0, 2, 1, 3).reshape(B, S, WIDTH_B)

        mixed = jnp.concatenate([rmsnorm(o_a, grp_norm_a[l]), rmsnorm(o_b, grp_norm_b[l])], axis=-1)
        h = h + mixed @ w_out[l]

        hn2 = rmsnorm(h, norm2_g[l])
        h = h + (jax.nn.silu(hn2 @ w_gate[l]) * (hn2 @ w_up[l])) @ w_down[l]
    return rmsnorm(h, final_g)
```

```python
import math
from contextlib import ExitStack

import numpy as np
import ml_dtypes

import concourse.bass as bass
import concourse.mybir as mybir
from concourse.bass_utils import run_bass_kernel_spmd

F32 = mybir.dt.float32
BF16 = mybir.dt.bfloat16
AF = mybir.ActivationFunctionType
ALU = mybir.AluOpType
AX = mybir.AxisListType

D = 2048
S = 4096
NQ = 2048
HD = 128
PIN = 5656
FFN = 5632
EPS = 1e-6
NEG = -30000.0
QSCALE = 1.0 / math.sqrt(128.0)
NBF = ml_dtypes.bfloat16


class Eng:
    EPOCH = 8000

    def __init__(self, k, name, h, track_self=True):
        self.k, self.name, self.h = k, name, h
        self.sems = []
        self.count = 0
        self.seen = {}
        self.track_self = track_self
        self._new_epoch()

    def _new_epoch(self):
        sem = self.k.es.enter_context(self.k.nc.semaphore(f"{self.name}_e{len(self.sems)}"))
        self.sems.append(sem)
        self.count = 0

    def next_event(self):
        if self.count >= self.EPOCH:
            self._new_epoch()
        return (self.sems[-1], self.count + 1)

    def wait(self, sem, val):
        if self.seen.get(id(sem), 0) >= val:
            return
        self.h.wait_ge(sem, val)
        self.seen[id(sem)] = val


class DmaSem:
    def __init__(self, sem):
        self.sem = sem
        self.count = 0


class K:
    def __init__(self):
        self.nc = bass.Bass("TRN2", target_bir_lowering=False)
        self.es = ExitStack()
        nc = self.nc
        self.pe = Eng(self, "pe", nc.tensor, track_self=False)
        self.act = Eng(self, "act", nc.scalar)
        self.dve = Eng(self, "dve", nc.vector)
        self.pool = Eng(self, "pool", nc.gpsimd)
        self.sp = Eng(self, "sp", nc.sync)
        self.lastw = {}
        self.readers = {}
        self.dsems = [DmaSem(self.es.enter_context(nc.semaphore(f"dma{i}"))) for i in range(32)]
        self.drr = 0
        self.psems = [DmaSem(self.es.enter_context(nc.semaphore(f"sdma{i}"))) for i in range(6)]
        self.prr = 0
        self.csems = [DmaSem(self.es.enter_context(nc.semaphore(f"cnv{i}"))) for i in range(4)]
        self.crr = 0
        self.pool_hist = []
        self.dram = {}
        self.nps = 0

    def sb(self, name, shape, dt):
        return self.es.enter_context(self.nc.sbuf_tensor(name, list(shape), dt))

    def ps(self, name):
        return self.es.enter_context(self.nc.psum_tensor(name, [128, 512], F32))

    def dt_in(self, name, shape, dt):
        t = self.nc.dram_tensor(name, list(shape), dt, kind="ExternalInput")
        self.dram[name] = t
        return t

    def dt_out(self, name, shape, dt):
        t = self.nc.dram_tensor(name, list(shape), dt, kind="ExternalOutput")
        self.dram[name] = t
        return t

    def dt_tmp(self, name, shape, dt, debug=False, shared=False):
        t = self.nc.dram_tensor(name, list(shape), dt, kind="Internal" if shared else "ExternalOutput")
        self.dram[name] = t
        return t

    def _deps(self, reads, writes):
        deps = {}

        def add(ev):
            if ev is None:
                return
            s, v = ev
            if deps.get(id(s), (None, 0))[1] < v:
                deps[id(s)] = (s, v)

        for b in reads:
            add(self.lastw.get(b))
        for b in writes:
            add(self.lastw.get(b))
            for ev in self.readers.get(b, {}).values():
                add(ev)
        return deps

    def _mark(self, ev, reads, writes):
        s, v = ev
        for b in reads:
            r = self.readers.setdefault(b, {})
            if r.get(id(s), (None, 0))[1] < v:
                r[id(s)] = ev
        for b in writes:
            self.lastw[b] = ev
            self.readers[b] = {}

    def op(self, eng, fn, reads=(), writes=(), inc=True):
        ev = eng.next_event()
        own = {id(s) for s in eng.sems}
        for sid, (s, v) in self._deps(reads, writes).items():
            if sid in own and not eng.track_self:
                continue
            eng.wait(s, v)
        ins = fn()
        if inc:
            ins.then_inc(ev[0], 1)
            eng.count += 1
        self._mark(ev, reads, writes)
        return ins

    def dma(self, q, out, in_, reads=(), writes=(), dram_only=False, **kw):
        if dram_only:
            ds = self.csems[self.crr]
            self.crr = (self.crr + 1) % len(self.csems)
        elif q is self.pool:
            ds = self.psems[self.prr]
            self.prr = (self.prr + 1) % len(self.psems)
        else:
            ds = self.dsems[self.drr]
            self.drr = (self.drr + 1) % len(self.dsems)
        q.wait(ds.sem, ds.count * 16)
        if q is self.pool:
            if len(self.pool_hist) >= 3:
                pe_ = self.pool_hist[-3]
                q.wait(pe_[0], pe_[1])
        for sid, (s, v) in self._deps(reads, writes).items():
            q.wait(s, v)
        q.h.dma_start(out=out, in_=in_, **kw).then_inc(ds.sem, 16)
        ds.count += 1
        ev = (ds.sem, ds.count * 16)
        if q is self.pool:
            self.pool_hist.append(ev)
        self._mark(ev, reads, writes)
        return ev

    def barrier(self):
        engs = [self.pe, self.act, self.dve, self.pool, self.sp]
        evs = [(e.sems[-1], e.count) for e in engs if e.count > 0]
        evs += [(d.sem, d.count * 16) for d in self.dsems + self.psems if d.count > 0]
        for e in engs:
            for (sem, v) in evs:
                e.wait(sem, v)

    def finish(self, keys):
        for b in keys:
            ev = self.lastw.get(b)
            if ev is not None:
                self.sp.wait(ev[0], ev[1])


def build(debug=False, phases=("p1", "p2", "p3"), mix_input=False):
    k = K()
    nc = k.nc
    pe, act, dve, pool, sp = k.pe, k.act, k.dve, k.pool, k.sp

    x = k.dt_in("x", [S, D], F32)
    w_in = k.dt_in("w_in", [D, PIN], F32)
    g1 = k.dt_in("g1", [1, D], F32)
    ident_d = k.dt_in("ident", [128, 128], BF16)
    biasA_d = k.dt_in("biasA", [8, 3, 2, 128, 256], F32)
    C = {}
    for nm, shp, dt in (("Ltab", [69, 32, 128], BF16), ("Lc", [4, 2, 128], BF16), ("maskLE", [128, 512], BF16),
                        ("maskGT", [128, 512], BF16), ("addtab", [128, 16, 64], F32), ("ovl", [128, 2, 64], BF16),
                        ("Rstat", [2, 16, 5, 512], BF16), ("Rc", [2, 16, 4, 512], BF16),
                        ("cmpmask", [16, 128, 2, 512], BF16),
                        ("pe_k", [32, 128], F32), ("pe_v", [32, 128], F32), ("w1_k", [4096, 256], F32),
                        ("w1_v", [4096, 256], F32), ("w2_k", [256, 128], F32), ("w2_v", [256, 128], F32)):
        C[nm] = k.dt_in(nm, shp, dt)
    out = k.dt_out("out", [NQ, D], F32)

    KTa = k.dt_tmp("KTa", [8, 128, S], BF16, debug)
    QTa = k.dt_tmp("QTa", [8, 128, NQ], BF16, debug)
    Va = k.dt_tmp("Va", [S, 1024], BF16, debug)
    QTb = k.dt_tmp("QTb", [8, 128, NQ], BF16, debug)
    KVc = k.dt_tmp("KVc", [4, 128, S], F32, debug)
    KTs = k.dt_tmp("KTs", [2, 128, S], BF16, debug)
    Vs = k.dt_tmp("Vs", [S, 256], BF16, debug)
    KTw = k.dt_tmp("KTw", [2, 128, S], BF16, debug)
    Vw = k.dt_tmp("Vw", [S, 256], BF16, debug)
    Gt = k.dt_tmp("Gt", [NQ, 24], F32, debug)

    W = {}
    for nm, shp in (("w_out", [D, D]), ("w_gate", [D, FFN]), ("w_up", [D, FFN]), ("w_down", [FFN, D]),
                    ("gab", [1, D]), ("g2", [1, D]), ("gf", [1, D])):
        W[nm] = k.dt_in(nm, shp, F32)
    W["W1c"] = k.dt_tmp("W1c", [2, 128, 32, 256], BF16, shared=True)
    W["W2c"] = k.dt_tmp("W2c", [2, 128, 2, 128], BF16, shared=True)
    W["WO"] = k.dt_tmp("WO", [4, 128, 16, 512], BF16, shared=True)
    W["WG"] = k.dt_tmp("WG", [11, 128, 16, 512], BF16, shared=True)
    W["WU"] = k.dt_tmp("WU", [11, 128, 16, 512], BF16, shared=True)
    W["WD"] = k.dt_tmp("WD", [4, 4, 128, 11, 512], BF16, shared=True)
    if mix_input:
        MIX = k.dt_in("MIX", [NQ, D], BF16)
    else:
        MIX = k.dt_tmp("MIX", [NQ, D], BF16, debug)

    PS = [k.ps(f"ps{i}") for i in range(8)]

    ident = k.sb("ident_sb", [128, 128], BF16)
    k.dma(sp, ident[:], ident_d.ap(), writes=["ident"])

    if "p1" in phases:
        phase1(k, x, w_in, g1, ident, PS, dict(KTa=KTa, QTa=QTa, Va=Va, QTb=QTb, KVc=KVc, KTs=KTs, Vs=Vs,
                                                 KTw=KTw, Vw=Vw, Gt=Gt))
    st = dict(psi=0)
    k.barrier()
    phase0(k, W, C, with_dense=("p3" in phases))
    if "p2a" in phases or "p2" in phases:
        phase2a(k, dict(KTa=KTa, QTa=QTa, Va=Va), ident, PS, MIX, biasA_d, st)
    k.barrier()
    if "p2b" in phases or "p2" in phases:
        DBG = None
        if debug:
            DBG = dict(kc=k.dt_out("DBGkc", [2, 128, 256], BF16), vc=k.dt_out("DBGvc", [2, 128, 2, 193], BF16),
                       o=k.dt_out("DBGo", [3, NQ, 1024], F32), score=k.dt_out("DBGscore", [2, 16, 128, 64], F32),
                       t1=k.dt_out("DBGt1", [2, 16, 128, 64], F32))
        C2 = dict(C)
        C2["W1c"], C2["W2c"] = W["W1c"], W["W2c"]
        phase2b(k, dict(KVc=KVc, KTs=KTs, Vs=Vs, KTw=KTw, Vw=Vw, QTb=QTb, Gt=Gt), ident, PS, MIX, C2, st, DBG)
    k.barrier()
    if "p3" in phases:
        phase3(k, x, W, ident, PS, MIX, out)

    k.finish(list(k.lastw.keys()))
    return k


def rsqrt_eps(k, sq, key=None, n=1):
    nc = k.nc
    ks = key
    k.op(k.dve, lambda: nc.vector.tensor_scalar(out=sq[:, n:2 * n], in0=sq[:, 0:n], scalar1=EPS, scalar2=None,
                                                op0=ALU.add), reads=[ks], writes=[ks])
    k.op(k.dve, lambda: nc.vector.reciprocal(out=sq[:, n:2 * n], in_=sq[:, n:2 * n]), reads=[ks], writes=[ks])
    k.op(k.act, lambda: nc.scalar.activation(out=sq[:, n:2 * n], in_=sq[:, n:2 * n], func=AF.Sqrt), reads=[ks],
         writes=[ks])


def phase0(k, W, C, with_dense=True):
    pool = k.pool
    for kv, nm in enumerate(("k", "v")):
        k.dma(pool, W["W1c"].ap()[kv], C[f"w1_{nm}"].ap().rearrange("(p d) c -> d p c", d=128), writes=[("W1c", kv)], dram_only=True)
        k.dma(pool, W["W2c"].ap()[kv], C[f"w2_{nm}"].ap().rearrange("(hc p) c -> p hc c", p=128), writes=[("W2c", kv)], dram_only=True)
    if not with_dense:
        return
    wo = W["w_out"].ap().rearrange("(kc p) (t c) -> t p kc c", p=128, c=512)
    for t in range(4):
        k.dma(pool, W["WO"].ap()[t], wo[t], writes=[("WO", t)], dram_only=True)
    wg = W["w_gate"].ap().rearrange("(kc p) (t c) -> t p kc c", p=128, c=512)
    wu = W["w_up"].ap().rearrange("(kc p) (t c) -> t p kc c", p=128, c=512)
    for t in range(11):
        k.dma(pool, W["WG"].ap()[t], wg[t], writes=[("WG", t)], dram_only=True)
        k.dma(pool, W["WU"].ap()[t], wu[t], writes=[("WU", t)], dram_only=True)
    wd = W["w_down"].ap().rearrange("(kq kc p) (cb c) -> cb kq p kc c", p=128, kc=11, c=512)
    for cb in range(4):
        for kq in range(4):
            k.dma(pool, W["WD"].ap()[cb, kq], wd[cb, kq], writes=[("WD", cb, kq)], dram_only=True)


def phase2a(k, T, ident, PS, MIX, biasA_d, st):
    nc = k.nc
    pe, act, dve, pool, sp = k.pe, k.act, k.dve, k.pool, k.sp
    es = ExitStack()

    def sb(name, shape, dt):
        return es.enter_context(nc.sbuf_tensor(name, list(shape), dt))

    KT = [sb(f"a_KT{i}", [128, S], BF16) for i in range(2)]
    QT = [sb(f"a_QT{i}", [128, NQ], BF16) for i in range(2)]
    KT4 = sb("a_KT4", [128, 4, 1024], BF16)
    KT16 = sb("a_KT16", [128, 16, 256], BF16)
    QT4 = sb("a_QT4", [128, 4, 512], BF16)
    QT16 = sb("a_QT16", [128, 16, 128], BF16)
    V1 = [sb(f"a_V1{i}", [128, 32, 128], BF16) for i in range(2)]
    V4 = [sb(f"a_V4{i}", [128, 8, 4, 128], BF16) for i in range(2)]
    V16 = [sb(f"a_V16{i}", [128, 2, 16, 128], BF16) for i in range(2)]
    bias = [sb(f"a_bias{i}", [128, 3, 2, 256], F32) for i in range(2)]
    acc2 = sb("a_acc2", [128, 2, NQ], F32)
    sT = [sb(f"a_sT{i}", [128, 256], F32) for i in range(3)]
    pT = [sb(f"a_pT{i}", [128, 256], BF16) for i in range(3)]
    ones = sb("a_ones", [128, 128], BF16)
    rec = sb("a_rec", [128, NQ], F32)
    oT = sb("a_oT", [128, NQ], BF16)
    ob = sb("a_ob", [128, 16, 128], BF16)
    k.op(dve, lambda: nc.vector.memset(ones[:], 1.0), writes=["a_ones"])

    def bank():
        i = st["psi"] % 8
        st["psi"] += 1
        return PS[i], f"ps{i}"

    ui = 0
    for h in range(8):
        b = h % 2
        kK, kQ, kV1, kV4, kV16, kB = f"aKT{b}", f"aQT{b}", f"aV1{b}", f"aV4{b}", f"aV16{b}", f"abias{b}"
        k.dma(sp, KT[b][:], T["KTa"].ap()[h], reads=[("KTa", h, 0), ("KTa", h, 1)], writes=[kK])
        k.dma(sp, QT[b][:], T["QTa"].ap()[h], reads=[("QTa", h, 1)], writes=[kQ])
        vsrc = T["Va"].ap()[:, h * 128:(h + 1) * 128]
        vreads = [("Va", p_, t_, (h // 4) * 512) for p_ in range(2) for t_ in range(16)]
        k.dma(sp, V1[b][:], vsrc.rearrange("(m kk) c -> kk m c", kk=128), reads=vreads, writes=[kV1])
        v4s = vsrc.rearrange("(m kk r) c -> r kk m c", kk=128, r=4)
        for r in range(4):
            k.dma(sp, V4[b][:, :, r, :], v4s[r], reads=vreads, writes=[kV4])
        v16s = vsrc.rearrange("(m kk r) c -> m kk r c", kk=128, r=16)
        for m in range(2):
            k.dma(sp, V16[b][:, m, :, :], v16s[m], reads=vreads, writes=[kV16])
        k.dma(sp, bias[b][:], biasA_d.ap()[h].rearrange("d v kk c -> kk d v c"), writes=[kB])
        k.op(act, lambda: nc.scalar.copy(out=KT4[:], in_=KT[b][:].rearrange("p (a r) -> p r a", r=4)),
             reads=[kK], writes=["aKT4"])
        k.op(act, lambda: nc.scalar.copy(out=KT16[:], in_=KT[b][:].rearrange("p (a r) -> p r a", r=16)),
             reads=[kK], writes=["aKT16"])
        k.op(act, lambda: nc.scalar.copy(out=QT4[:], in_=QT[b][:].rearrange("p (a r) -> p r a", r=4)),
             reads=[kQ], writes=["aQT4"])
        k.op(act, lambda: nc.scalar.copy(out=QT16[:], in_=QT[b][:].rearrange("p (a r) -> p r a", r=16)),
             reads=[kQ], writes=["aQT16"])
        units = []
        for n in range(16, 32):
            qt = n - 16
            units.append((0, n == 16, KT[b][:, (n - 1) * 128:n * 128], KT[b][:, n * 128:(n + 1) * 128],
                          QT[b][:, qt * 128:(qt + 1) * 128], V1[b][:, n - 1, :], V1[b][:, n, :],
                          slice(qt * 128, (qt + 1) * 128), [kK, kQ, kV1], True))
        for r in range(4):
            for n in range(4, 8):
                units.append((1, n == 4, KT4[:, r, (n - 1) * 128:n * 128], KT4[:, r, n * 128:(n + 1) * 128],
                              QT4[:, r, (n - 4) * 128:(n - 3) * 128], V4[b][:, n - 1, r, :], V4[b][:, n, r, :],
                              slice(512 * (n - 4) + r, 512 * (n - 4) + 512, 4), ["aKT4", "aQT4", kV4], False))
        for r in range(16):
            units.append((2, True, KT16[:, r, 0:128], KT16[:, r, 128:256], QT16[:, r, :], V16[b][:, 0, r, :],
                          V16[b][:, 1, r, :], slice(r, NQ, 16), ["aKT16", "aQT16", kV16], False))
        nu = len(units)
        live = {}
        for t in range(nu + 3):
            if t < nu:
                (di, first, Kp, Kd, Qn, Vp, Vd, osl, rk, is_copy) = units[t]
                b1, b1k = bank()
                k.op(pe, lambda: nc.tensor.matmul(b1[:, 0:128], lhsT=Kp, rhs=Qn, start=True, stop=True), reads=rk,
                     writes=[b1k], inc=False)
                k.op(pe, lambda: nc.tensor.matmul(b1[:, 128:256], lhsT=Kd, rhs=Qn, start=True, stop=True), reads=rk,
                     writes=[b1k])
                live[t] = dict(b1=b1, b1k=b1k, ui=ui)
                ui += 1
            u = t - 1
            if 0 <= u < nu:
                (di, first, Kp, Kd, Qn, Vp, Vd, osl, rk, is_copy) = units[u]
                L = live[u]
                sTb, sk = sT[L["ui"] % 3], f"asT{L['ui'] % 3}"
                pTb, pk = pT[L["ui"] % 3], f"apT{L['ui'] % 3}"
                b1, b1k = L["b1"], L["b1k"]
                k.op(dve, lambda: nc.vector.tensor_tensor(out=sTb[:], in0=b1[:, 0:256],
                                                          in1=bias[b][:, di, 1 if first else 0, :], op=ALU.add),
                     reads=[b1k, kB], writes=[sk])
                k.op(act, lambda: nc.scalar.activation(out=pTb[:], in_=sTb[:], func=AF.Exp), reads=[sk], writes=[pk])
            u = t - 2
            if 0 <= u < nu:
                (di, first, Kp, Kd, Qn, Vp, Vd, osl, rk, is_copy) = units[u]
                L = live[u]
                pTb, pk = pT[L["ui"] % 3], f"apT{L['ui'] % 3}"
                b2, b2k = bank()
                L["b2"], L["b2k"] = b2, b2k
                k.op(pe, lambda: nc.tensor.matmul(b2[:, 0:128], lhsT=Vp, rhs=pTb[:, 0:128], start=True, stop=False),
                     reads=rk + [pk], writes=[b2k], inc=False)
                k.op(pe, lambda: nc.tensor.matmul(b2[:, 0:128], lhsT=Vd, rhs=pTb[:, 128:256], start=False, stop=True),
                     reads=rk + [pk], writes=[b2k], inc=False)
                k.op(pe, lambda: nc.tensor.matmul(b2[:, 128:256], lhsT=ones[:], rhs=pTb[:, 0:128], start=True,
                                                  stop=False), reads=["a_ones", pk], writes=[b2k], inc=False)
                k.op(pe, lambda: nc.tensor.matmul(b2[:, 128:256], lhsT=ones[:], rhs=pTb[:, 128:256], start=False,
                                                  stop=True), reads=["a_ones", pk], writes=[b2k])
            u = t - 3
            if 0 <= u < nu:
                (di, first, Kp, Kd, Qn, Vp, Vd, osl, rk, is_copy) = units[u]
                L = live.pop(u)
                b2, b2k = L["b2"], L["b2k"]
                src = b2[:, 0:256].rearrange("p (a q) -> p a q", a=2)
                dst = acc2[:, :, osl]
                if is_copy:
                    k.op(dve, lambda: nc.vector.tensor_copy(out=dst, in_=src), reads=[b2k], writes=["a_acc2"])
                else:
                    k.op(dve, lambda: nc.vector.tensor_tensor(out=dst, in0=src, in1=dst, op=ALU.add),
                         reads=[b2k, "a_acc2"], writes=["a_acc2"])
        k.op(dve, lambda: nc.vector.reciprocal(out=rec[:], in_=acc2[:, 1, :]), reads=["a_acc2"], writes=["a_rec"])
        k.op(dve, lambda: nc.vector.tensor_tensor(out=oT[:], in0=acc2[:, 0, :], in1=rec[:], op=ALU.mult),
             reads=["a_acc2", "a_rec"], writes=["a_oT"])
        for half in range(2):
            bt, btk = bank()
            pv = bt[:].bitcast(BF16)
            for j in range(8):
                t = half * 8 + j
                k.op(pe, lambda: nc.tensor.transpose(out=pv[:, j * 128:(j + 1) * 128], in_=oT[:, t * 128:(t + 1) * 128],
                                                     identity=ident[:]),
                     reads=["a_oT", "ident"], writes=[btk], inc=(j == 7))
            k.op(act, lambda: nc.scalar.copy(out=ob[:, half * 8:half * 8 + 8, :],
                                             in_=pv.rearrange("p (a b) -> p a b", a=8)), reads=[btk], writes=["a_ob"])
        k.dma(sp, MIX.ap()[:, h * 128:(h + 1) * 128].rearrange("(t p) c -> p t c", p=128), ob[:], reads=["a_ob"],
              writes=[("MIXa", h)])
    es.close()


def phase2b(k, T, ident, PS, MIX, C, st, DBG=None):
    nc = k.nc
    pe, act, dve, pool, sp = k.pe, k.act, k.dve, k.pool, k.sp
    es = ExitStack()

    def sb(name, shape, dt):
        return es.enter_context(nc.sbuf_tensor(name, list(shape), dt))

    def bank():
        i = st["psi"] % 8
        st["psi"] += 1
        return PS[i], f"ps{i}"

    identf = sb("b_identf", [128, 128], F32)
    k.op(dve, lambda: nc.vector.tensor_copy(out=identf[:], in_=ident[:]), reads=["ident"], writes=["identf"])
    Ltab = sb("b_Ltab", [69, 32, 128], BF16)
    k.dma(sp, Ltab[:], C["Ltab"].ap(), writes=["Ltab"])
    Lc = sb("b_Lc", [4, 2, 128], BF16)
    k.dma(sp, Lc[:], C["Lc"].ap(), writes=["Lc"])
    maskLE = sb("b_maskLE", [128, 512], BF16)
    k.dma(sp, maskLE[:], C["maskLE"].ap(), writes=["maskLE"])
    maskGT = sb("b_maskGT", [128, 512], BF16)
    k.dma(sp, maskGT[:], C["maskGT"].ap(), writes=["maskGT"])
    addtab = sb("b_addtab", [128, 16, 64], F32)
    k.dma(sp, addtab[:], C["addtab"].ap(), writes=["addtab"])
    gsb = sb("b_gsb", [128, 16, 24], F32)
    k.dma(sp, gsb[:], T["Gt"].ap().rearrange("(i q) c -> q i c", q=128), reads=[("Gt", t) for t in range(16)],
          writes=["gsb"])
    kcT = [sb(f"b_kcT{g}", [128, 256], BF16) for g in range(2)]
    vcaug = [sb(f"b_vcaug{g}", [128, 2, 193], BF16) for g in range(2)]
    for g in range(2):
        k.op(dve, lambda: nc.vector.memset(vcaug[g][:, :, 128:129], 1.0), writes=[f"vcaug{g}"])
        k.dma(sp, vcaug[g][:, :, 129:193], C["ovl"].ap(), writes=[f"vcaug{g}"])

    es2 = ExitStack()

    def sb2(name, shape, dt):
        return es2.enter_context(nc.sbuf_tensor(name, list(shape), dt))

    kvT = sb2("c_kvT", [128, S], F32)
    pe_sb = sb2("c_pe", [32, 128], F32)
    peT = sb2("c_peT", [128, 32], F32)
    blocksT = sb2("c_blk", [128, 32, 255], BF16)
    W1sb = sb2("c_W1", [128, 32, 256], BF16)
    W2sb = sb2("c_W2", [128, 2, 128], BF16)
    hidT = sb2("c_hid", [128, 2, 256], BF16)
    k.op(dve, lambda: nc.vector.memset(hidT[:], 0.0), writes=["hidT"])
    for kv in range(2):
        nm = "k" if kv == 0 else "v"
        k.dma(sp, pe_sb[:], C[f"pe_{nm}"].ap(), writes=["pe_sb"])
        k.dma(sp, W1sb[:], C["W1c"].ap()[kv], reads=[("W1c", kv)], writes=["W1sb"])
        k.dma(sp, W2sb[:], C["W2c"].ap()[kv], reads=[("W2c", kv)], writes=["W2sb"])
        bt, btk = bank()
        k.op(pe, lambda: nc.tensor.transpose(out=bt[:, 0:32], in_=pe_sb[:, :], identity=identf[0:32, 0:32]),
             reads=["pe_sb", "identf"], writes=[btk])
        k.op(dve, lambda: nc.vector.tensor_copy(out=peT[:], in_=bt[:, 0:32]), reads=[btk], writes=["peT"])
        for g in range(2):
            idx = kv * 2 + g
            k.dma(sp, kvT[:], T["KVc"].ap()[idx], reads=[("KVc", idx, 0), ("KVc", idx, 1)], writes=["kvT"])
            for p in range(32):
                k.op(dve, lambda: nc.vector.tensor_scalar(out=blocksT[:, p, :], in0=kvT[:, p:p + 16 * 254 + 1:16],
                                                          scalar1=peT[:, p:p + 1], scalar2=None, op0=ALU.add),
                     reads=["kvT", "peT"], writes=["blocksT"])
            for hc in range(2):
                bh, bhk = bank()
                for p in range(32):
                    k.op(pe, lambda: nc.tensor.matmul(bh[:, 0:255], lhsT=W1sb[:, p, hc * 128:(hc + 1) * 128],
                                                      rhs=blocksT[:, p, :], start=(p == 0), stop=(p == 31)),
                         reads=["W1sb", "blocksT"], writes=[bhk], inc=(p == 31))
                k.op(act, lambda: nc.scalar.activation(out=hidT[:, hc, 0:255], in_=bh[:, 0:255], func=AF.Silu),
                     reads=[bhk], writes=["hidT"])
            if kv == 0:
                bo, bok = bank()
                for hc in range(2):
                    k.op(pe, lambda: nc.tensor.matmul(bo[:, 0:256], lhsT=W2sb[:, hc, :], rhs=hidT[:, hc, :],
                                                      start=(hc == 0), stop=(hc == 1)),
                         reads=["W2sb", "hidT"], writes=[bok], inc=(hc == 1))
                k.op(dve, lambda: nc.vector.tensor_copy(out=kcT[g][:], in_=bo[:, 0:256]), reads=[bok], writes=[f"kcT{g}"])
            else:
                for jc in range(2):
                    bo, bok = bank()
                    for hc in range(2):
                        k.op(pe, lambda: nc.tensor.matmul(bo[:, 0:128], lhsT=hidT[:, hc, jc * 128:(jc + 1) * 128],
                                                          rhs=W2sb[:, hc, :], start=(hc == 0), stop=(hc == 1)),
                             reads=["W2sb", "hidT"], writes=[bok], inc=(hc == 1))
                    k.op(dve, lambda: nc.vector.tensor_copy(out=vcaug[g][:, jc, 0:128], in_=bo[:, 0:128]), reads=[bok],
                         writes=[f"vcaug{g}"])
    es2.close()
    k.barrier()

    if DBG is not None:
        for g in range(2):
            k.dma(sp, DBG["kc"].ap()[g], kcT[g][:], reads=[f"kcT{g}"], writes=[("dbgkc", g)])
            k.dma(sp, DBG["vc"].ap()[g], vcaug[g][:], reads=[f"vcaug{g}"], writes=[("dbgvc", g)])
        dbgst = sb("b_dbgst", [128, 512], F32)

    def dbg_dump(br, g, i):
        if DBG is None:
            return
        k.op(k.act, lambda: nc.scalar.copy(out=dbgst[:], in_=mixacc[:]), reads=[("mixacc", h_) for h_ in range(4)], writes=["dbgst"])
        k.dma(sp, DBG["o"].ap()[br, i * 128:(i + 1) * 128, g * 512:(g + 1) * 512], dbgst[:], reads=["dbgst"],
              writes=[("dbgo", br, g, i)])

    KTs = sb("b_KTs", [128, S], BF16)
    KTw = sb("b_KTw", [128, S], BF16)
    Vsa = sb("b_Vsa", [128, 32, 129], BF16)
    Vwa = sb("b_Vwa", [128, 32, 129], BF16)
    k.op(dve, lambda: nc.vector.memset(Vsa[:, :, 128:129], 1.0), writes=["Vsa"])
    k.op(dve, lambda: nc.vector.memset(Vwa[:, :, 128:129], 1.0), writes=["Vwa"])
    Qg = sb("b_Qg", [128, 16, 4, 128], BF16)
    PTs = sb("b_PTs", [128, 32, 512], BF16)
    PTw = sb("b_PTw", [128, 5, 512], BF16)
    PTc = sb("b_PTc", [128, 2, 512], BF16)
    cm = [sb(f"b_cm{i}", [128, 2, 512], BF16) for i in range(2)]
    Rsel = [sb(f"b_Rsel{i}", [69, 512], BF16) for i in range(2)]
    Rc = [sb(f"b_Rc{i}", [4, 512], BF16) for i in range(2)]
    mixacc = sb("b_mixacc", [128, 512], F32)
    mixbf = [sb(f"b_mixbf{i}", [128, 512], BF16) for i in range(2)]
    impacc = sb("b_imp", [128, 64], F32)
    score = sb("b_score", [128, 64], F32)
    work = sb("b_work", [128, 64], F32)
    t1 = sb("b_t1", [128, 64], F32)
    t2 = sb("b_t2", [128, 64], F32)
    m8 = sb("b_m8", [128, 16], F32)
    sbq = sb("b_sbq", [128, 64], BF16)
    dn = sb("b_dn", [128, 4], F32)
    dn4 = sb("b_dn4", [128, 12], F32)
    imp4 = sb("b_imp4", [128, 4, 64], F32)

    vs_reads = [("Vs", p_, t_, 0) for p_ in range(2) for t_ in range(16)]
    vw_reads = [("Vw", p_, t_, 0) for p_ in range(2) for t_ in range(16)]
    def issue_tables(g_, i_, b_):
        k.dma(sp, Rsel[b_][64:69, :], C["Rstat"].ap()[g_, i_], writes=[f"Rsel{b_}"])
        k.dma(sp, Rc[b_][:], C["Rc"].ap()[g_, i_], writes=[f"Rc{b_}"])
        k.dma(sp, cm[b_][:], C["cmpmask"].ap()[i_], writes=[f"cm{b_}"])

    it = 0
    for g in range(2):
        k.dma(sp, KTs[:], T["KTs"].ap()[g], reads=[("KTs", g, 0), ("KTs", g, 1)], writes=["KTs"])
        k.dma(sp, KTw[:], T["KTw"].ap()[g], reads=[("KTw", g, 0), ("KTw", g, 1)], writes=["KTw"])
        k.dma(sp, Vsa[:, :, 0:128], T["Vs"].ap()[:, g * 128:(g + 1) * 128].rearrange("(j kk) c -> kk j c", kk=128),
              reads=vs_reads, writes=["Vsa"])
        k.dma(sp, Vwa[:, :, 0:128], T["Vw"].ap()[:, g * 128:(g + 1) * 128].rearrange("(j kk) c -> kk j c", kk=128),
              reads=vw_reads, writes=["Vwa"])
        for hq in range(4):
            k.dma(sp, Qg[:, :, hq, :], T["QTb"].ap()[g * 4 + hq].rearrange("d (i q) -> d i q", q=128),
                  reads=[("QTb", g * 4 + hq, 1)], writes=["Qg"])
        for i in range(16):
            b = it % 2
            if it == 0:
                issue_tables(0, 0, 0)
            nxt = it + 1
            if nxt < 32:
                issue_tables(nxt // 16, nxt % 16, nxt % 2)
            it += 1
            kR, kRc, kcm, kmb = f"Rsel{b}", f"Rc{b}", f"cm{b}", f"mixbf{b}"
            qrhs = Qg[:, i, :, :]

            def coef_for(bo, bok, col, gcol):
                k.op(dve, lambda: nc.vector.tensor_scalar(out=dn[:, 0:1], in0=bo[:, col:col + 1], scalar1=1e-30,
                                                          scalar2=None, op0=ALU.max), reads=[bok], writes=["dn"])
                k.op(dve, lambda: nc.vector.reciprocal(out=dn[:, 1:2], in_=dn[:, 0:1]), reads=["dn"], writes=["dn"])
                k.op(dve, lambda: nc.vector.tensor_tensor(out=dn[:, 2:3], in0=dn[:, 1:2], in1=gsb[:, i, gcol:gcol + 1],
                                                          op=ALU.mult), reads=["dn", "gsb"], writes=["dn"])

            for jc in range(2):
                bs, bsk = bank()
                k.op(pe, lambda: nc.tensor.matmul(bs[:], lhsT=kcT[g][:, jc * 128:(jc + 1) * 128], rhs=qrhs,
                                                  start=True, stop=False), reads=[f"kcT{g}", "Qg"], writes=[bsk], inc=False)
                k.op(pe, lambda: nc.tensor.matmul(bs[:], lhsT=Lc[0:4, jc, :], rhs=Rc[b][0:4, :], start=False, stop=False),
                     reads=["Lc", kRc], writes=[bsk], inc=False)
                k.op(pe, lambda: nc.tensor.matmul(bs[:], lhsT=ident[:], rhs=cm[b][:, jc, :], start=False, stop=True),
                     reads=["ident", kcm], writes=[bsk])
                k.op(act, lambda: nc.scalar.activation(out=PTc[:, jc, :], in_=bs[:], func=AF.Exp), reads=[bsk],
                     writes=["PTc"])
            bo_o, bok_o = bank()
            bo_d, bok_d = bank()
            for hq in range(4):
                for jc in range(2):
                    k.op(pe, lambda: nc.tensor.matmul(bo_o[:, hq * 128:(hq + 1) * 128],
                                                      lhsT=PTc[:, jc, hq * 128:(hq + 1) * 128], rhs=vcaug[g][:, jc, 0:128],
                                                      start=(jc == 0), stop=(jc == 1)),
                         reads=["PTc", f"vcaug{g}"], writes=[bok_o], inc=(hq == 3 and jc == 1))
            for hq in range(4):
                for jc in range(2):
                    k.op(pe, lambda: nc.tensor.matmul(bo_d[:, hq * 65:(hq + 1) * 65],
                                                      lhsT=PTc[:, jc, hq * 128:(hq + 1) * 128], rhs=vcaug[g][:, jc, 128:193],
                                                      start=(jc == 0), stop=(jc == 1)),
                         reads=["PTc", f"vcaug{g}"], writes=[bok_d], inc=(hq == 3 and jc == 1))
            dv = bo_d[:, 0:260].rearrange("p (h c) -> p h c", h=4)
            k.op(dve, lambda: nc.vector.tensor_scalar(out=dn4[:, 0:4], in0=dv[:, :, 0], scalar1=1e-30, scalar2=None,
                                                      op0=ALU.max), reads=[bok_d], writes=["dn4a"])
            k.op(dve, lambda: nc.vector.reciprocal(out=dn4[:, 4:8], in_=dn4[:, 0:4]), reads=["dn4a"], writes=["dn4b"])
            k.op(dve, lambda: nc.vector.tensor_scalar(out=imp4[:, 1, :], in0=dv[:, 1, 1:65], scalar1=dn4[:, 5:6],
                                                      scalar2=None, op0=ALU.mult), reads=[bok_d, "dn4b"], writes=["imp_m1"])
            k.op(dve, lambda: nc.vector.tensor_scalar(out=imp4[:, 3, :], in0=dv[:, 3, 1:65], scalar1=dn4[:, 7:8],
                                                      scalar2=None, op0=ALU.mult), reads=[bok_d, "dn4b"], writes=["imp_m3"])
            k.op(dve, lambda: nc.vector.scalar_tensor_tensor(out=imp4[:, 0, :], in0=dv[:, 0, 1:65], scalar=dn4[:, 4:5],
                                                             in1=imp4[:, 1, :], op0=ALU.mult, op1=ALU.add),
                 reads=[bok_d, "dn4b", "imp_m1"], writes=["imp_s01"])
            k.op(dve, lambda: nc.vector.scalar_tensor_tensor(out=imp4[:, 2, :], in0=dv[:, 2, 1:65], scalar=dn4[:, 6:7],
                                                             in1=imp4[:, 3, :], op0=ALU.mult, op1=ALU.add),
                 reads=[bok_d, "dn4b", "imp_m3"], writes=["imp_s23"])
            k.op(dve, lambda: nc.vector.tensor_tensor(out=impacc[:], in0=imp4[:, 0, :], in1=imp4[:, 2, :], op=ALU.add),
                 reads=["imp_s01", "imp_s23"], writes=["impacc"])
            dbg_dump(0, g, i)
            k.op(dve, lambda: nc.vector.tensor_tensor(out=score[:], in0=impacc[:], in1=addtab[:, i, :], op=ALU.add),
                 reads=["impacc", "addtab"], writes=["score"])
            k.op(dve, lambda: nc.vector.max(out=m8[:, 0:8], in_=score[:]), reads=["score"], writes=["m8"])
            k.op(dve, lambda: nc.vector.match_replace(out=work[:], in_to_replace=m8[:, 0:8], in_values=score[:],
                                                      imm_value=-1e9), reads=["score", "m8"], writes=["work"])
            k.op(dve, lambda: nc.vector.max(out=m8[:, 8:16], in_=work[:]), reads=["work", "m8"], writes=["m8"])
            k.op(dve, lambda: nc.vector.tensor_scalar(out=t1[:], in0=score[:], scalar1=m8[:, 15:16], scalar2=None,
                                                      op0=ALU.is_ge), reads=["score", "m8"], writes=["t1"])
            k.op(dve, lambda: nc.vector.tensor_scalar(out=t2[:], in0=score[:], scalar1=0.0, scalar2=None,
                                                      op0=ALU.is_ge), reads=["score"], writes=["t2"])
            k.op(dve, lambda: nc.vector.tensor_tensor(out=t1[:], in0=t1[:], in1=t2[:], op=ALU.mult),
                 reads=["t1", "t2"], writes=["t1"])
            k.op(dve, lambda: nc.vector.tensor_scalar(out=sbq[:], in0=t1[:], scalar1=30000.0, scalar2=-30000.0,
                                                      op0=ALU.mult, op1=ALU.add), reads=["t1"], writes=["sbq"])
            if DBG is not None:
                k.dma(sp, DBG["score"].ap()[g, i], score[:], reads=["score"], writes=[("dbgsc", g, i)])
                k.dma(sp, DBG["t1"].ap()[g, i], t1[:], reads=["t1"], writes=[("dbgt1", g, i)])
            k.op(dve, lambda: nc.vector.tensor_tensor(out=dn4[:, 8:12], in0=dn4[:, 4:8],
                                                      in1=gsb[:, i, g * 12:g * 12 + 10:3], op=ALU.mult),
                 reads=["dn4b", "gsb"], writes=["dn4c"])
            for hq in range(4):
                k.op(dve, lambda: nc.vector.tensor_scalar(out=mixacc[:, hq * 128:(hq + 1) * 128],
                                                          in0=bo_o[:, hq * 128:(hq + 1) * 128], scalar1=dn4[:, 8 + hq:9 + hq],
                                                          scalar2=None, op0=ALU.mult),
                     reads=[bok_o, "dn4c"], writes=[("mixacc", hq)])
            bt, btk = bank()
            pv = bt[:].bitcast(BF16)
            k.op(pe, lambda: nc.tensor.transpose(out=pv[0:64, 0:128], in_=sbq[:, 0:64], identity=ident[:]),
                 reads=["sbq", "ident"], writes=[btk])
            for hq in range(4):
                k.op(dve, lambda: nc.vector.tensor_copy(out=Rsel[b][0:64, hq * 128:(hq + 1) * 128], in_=pv[0:64, 0:128]),
                     reads=[btk], writes=[(kR, hq)])
            nj = 17 + i
            for j in range(nj):
                bs, bsk = bank()
                last = (j == nj - 1)
                k.op(pe, lambda: nc.tensor.matmul(bs[:], lhsT=KTs[:, j * 128:(j + 1) * 128], rhs=qrhs, start=True,
                                                  stop=False), reads=["KTs", "Qg"], writes=[bsk], inc=False)
                k.op(pe, lambda: nc.tensor.matmul(bs[:], lhsT=Ltab[0:69, j, :], rhs=Rsel[b][0:69, :], start=False,
                                                  stop=not last), reads=["Ltab", kR] + [(kR, h_) for h_ in range(4)],
                     writes=[bsk], inc=not last)
                if last:
                    k.op(pe, lambda: nc.tensor.matmul(bs[:], lhsT=ident[:], rhs=maskLE[:], start=False, stop=True),
                         reads=["ident", "maskLE"], writes=[bsk])
                k.op(act, lambda: nc.scalar.activation(out=PTs[:, j, :], in_=bs[:], func=AF.Exp), reads=[bsk],
                     writes=[("PTs", j)])
            for hq in range(4):
                bo, bok = bank()
                for j in range(nj):
                    k.op(pe, lambda: nc.tensor.matmul(bo[:, 0:129], lhsT=PTs[:, j, hq * 128:(hq + 1) * 128],
                                                      rhs=Vsa[:, j, :], start=(j == 0), stop=(j == nj - 1)),
                         reads=[("PTs", j), "Vsa"], writes=[bok], inc=(j == nj - 1))
                coef_for(bo, bok, 128, (g * 4 + hq) * 3 + 1)
                ms = mixacc[:, hq * 128:(hq + 1) * 128]
                k.op(dve, lambda: nc.vector.scalar_tensor_tensor(out=ms, in0=bo[:, 0:128], scalar=dn[:, 2:3], in1=ms,
                                                                 op0=ALU.mult, op1=ALU.add),
                     reads=[bok, "dn", ("mixacc", hq)], writes=[("mixacc", hq)])
            dbg_dump(1, g, i)
            for jj in range(5):
                j = 12 + i + jj
                bs, bsk = bank()
                edge = jj in (0, 4)
                k.op(pe, lambda: nc.tensor.matmul(bs[:], lhsT=KTw[:, j * 128:(j + 1) * 128], rhs=qrhs, start=True,
                                                  stop=False), reads=["KTw", "Qg"], writes=[bsk], inc=False)
                k.op(pe, lambda: nc.tensor.matmul(bs[:], lhsT=Ltab[64:69, j, :], rhs=Rsel[b][64:69, :], start=False,
                                                  stop=not edge), reads=["Ltab", kR], writes=[bsk], inc=not edge)
                if edge:
                    mk, mkk = (maskGT, "maskGT") if jj == 0 else (maskLE, "maskLE")
                    k.op(pe, lambda: nc.tensor.matmul(bs[:], lhsT=ident[:], rhs=mk[:], start=False, stop=True),
                         reads=["ident", mkk], writes=[bsk])
                k.op(act, lambda: nc.scalar.activation(out=PTw[:, jj, :], in_=bs[:], func=AF.Exp), reads=[bsk],
                     writes=["PTw"])
            for hq in range(4):
                bo, bok = bank()
                for jj in range(5):
                    j = 12 + i + jj
                    k.op(pe, lambda: nc.tensor.matmul(bo[:, 0:129], lhsT=PTw[:, jj, hq * 128:(hq + 1) * 128],
                                                      rhs=Vwa[:, j, :], start=(jj == 0), stop=(jj == 4)),
                         reads=["PTw", "Vwa"], writes=[bok], inc=(jj == 4))
                coef_for(bo, bok, 128, (g * 4 + hq) * 3 + 2)
                ms = mixacc[:, hq * 128:(hq + 1) * 128]
                k.op(dve, lambda: nc.vector.scalar_tensor_tensor(out=ms, in0=bo[:, 0:128], scalar=dn[:, 2:3], in1=ms,
                                                                 op0=ALU.mult, op1=ALU.add),
                     reads=[bok, "dn", ("mixacc", hq)], writes=[("mixacc", hq)])
            dbg_dump(2, g, i)
            k.op(act, lambda: nc.scalar.copy(out=mixbf[b][:], in_=mixacc[:]), reads=[("mixacc", h_) for h_ in range(4)], writes=[kmb])
            k.dma(sp, MIX.ap()[i * 128:(i + 1) * 128, 1024 + g * 512:1024 + (g + 1) * 512], mixbf[b][:], reads=[kmb],
                  writes=[("MIXb", g, i)])
    es.close()


def phase3(k, x, W, ident, PS, MIX, out):
    nc = k.nc
    pe, act, dve, pool, sp = k.pe, k.act, k.dve, k.pool, k.sp
    es = ExitStack()

    def sb(name, shape, dt):
        return es.enter_context(nc.sbuf_tensor(name, list(shape), dt))

    mx = sb("p3_mx", [128, D], BF16)
    mn = sb("p3_mn", [128, D], BF16)
    junk = sb("p3_junk", [128, D], BF16)
    gbc = sb("p3_gbc", [128, D], F32)
    sq = sb("p3_sq", [128, 4], F32)
    tT = sb("p3_tT", [128, 16, 512], BF16)
    h1 = sb("p3_h1", [128, 4, D], F32)
    actT = sb("p3_actT", [128, 44, 512], BF16)
    wt = [sb(f"p3_wt{i}", [128, 16, 512], BF16) for i in range(3)]
    sg = [sb(f"p3_sg{i}", [128, 512], F32) for i in range(2)]
    outt = sb("p3_outt", [128, D], F32)
    st = dict(psi=0, wi=0, evi=0, sgi=0)

    def bank():
        i = st["psi"] % 8
        st["psi"] += 1
        return PS[i], f"ps{i}"

    def wtile():
        i = st["wi"] % 3
        st["wi"] += 1
        return wt[i], f"p3wt{i}"

    def transpose_to(src, skey, tl):
        for half in range(2):
            bk_t, bk = bank()
            pv = bk_t[:].bitcast(BF16)
            for j in range(8):
                kc = half * 8 + j
                k.op(pe, lambda: nc.tensor.transpose(out=pv[:, j * 128:(j + 1) * 128],
                                                     in_=src[:, kc * 128:(kc + 1) * 128], identity=ident[:]),
                     reads=[skey, "ident"], writes=[bk], inc=(j == 7))
            dst = tT[:, half * 8:half * 8 + 8, tl * 128:(tl + 1) * 128]
            srcp = pv.rearrange("p (a b) -> p a b", a=8)
            if st["evi"] % 2 == 0:
                k.op(act, lambda: nc.scalar.copy(out=dst, in_=srcp), reads=[bk], writes=[("tT", tl)])
            else:
                k.op(dve, lambda: nc.vector.tensor_copy(out=dst, in_=srcp), reads=[bk], writes=[("tT", tl)])
            st["evi"] += 1

    x_v = x.ap().rearrange("(t p) f -> t p f", p=128)
    mix_v = MIX.ap().rearrange("(t p) f -> t p f", p=128)
    out_v = out.ap().rearrange("(t p) f -> t p f", p=128)
    tTkeys = [("tT", t) for t in range(4)]

    for tb in range(4):
        k.dma(sp, gbc[:], W["gab"].ap().to_broadcast([128, D]), writes=["gbc"])
        for tl in range(4):
            tt = tb * 4 + tl
            k.dma(sp, mx[:], mix_v[tt], reads=[("MIXa", h_) for h_ in range(8)] + [("MIXb", g_, tt) for g_ in range(2)],
                  writes=["mx"])
            k.dma(sp, h1[:, tl, :], x_v[16 + tt], writes=[("h1", tl)])
            for hf in range(2):
                k.op(act, lambda: nc.scalar.activation(out=junk[:, 0:1024], in_=mx[:, hf * 1024:(hf + 1) * 1024],
                                                       func=AF.Square, scale=1.0 / 32.0, accum_out=sq[:, hf:hf + 1]),
                     reads=["mx"], writes=["junk", "sq"])
            rsqrt_eps(k, sq, "sq", 2)
            for hf in range(2):
                k.op(dve, lambda: nc.vector.scalar_tensor_tensor(out=mn[:, hf * 1024:(hf + 1) * 1024],
                                                                 in0=mx[:, hf * 1024:(hf + 1) * 1024],
                                                                 scalar=sq[:, 2 + hf:3 + hf],
                                                                 in1=gbc[:, hf * 1024:(hf + 1) * 1024],
                                                                 op0=ALU.mult, op1=ALU.mult),
                     reads=["mx", "sq", "gbc"], writes=["mn"])
            transpose_to(mn, "mn", tl)
        for cb in range(4):
            wb, wk = wtile()
            k.dma(sp, wb[:], W["WO"].ap()[cb], reads=[("WO", cb)], writes=[wk])
            for tl in range(4):
                bk_t, bk = bank()
                for kc in range(16):
                    k.op(pe, lambda: nc.tensor.matmul(bk_t[:], lhsT=tT[:, kc, tl * 128:(tl + 1) * 128], rhs=wb[:, kc, :],
                                                      start=(kc == 0), stop=(kc == 15)),
                         reads=[wk, ("tT", tl)], writes=[bk], inc=(kc == 15))
                hs = h1[:, tl, cb * 512:(cb + 1) * 512]
                k.op(dve, lambda: nc.vector.tensor_tensor(out=hs, in0=bk_t[:], in1=hs, op=ALU.add),
                     reads=[bk, ("h1", tl)], writes=[("h1", tl)])
        k.dma(sp, gbc[:], W["g2"].ap().to_broadcast([128, D]), writes=["gbc"])
        for tl in range(4):
            k.op(act, lambda: nc.scalar.activation(out=junk[:], in_=h1[:, tl, :], func=AF.Square,
                                                   scale=1.0 / math.sqrt(D), accum_out=sq[:, 0:1]),
                 reads=[("h1", tl)], writes=["junk", "sq"])
            rsqrt_eps(k, sq, "sq", 1)
            k.op(dve, lambda: nc.vector.scalar_tensor_tensor(out=mn[:], in0=h1[:, tl, :], scalar=sq[:, 1:2], in1=gbc[:],
                                                             op0=ALU.mult, op1=ALU.mult),
                 reads=[("h1", tl), "sq", "gbc"], writes=["mn"])
            transpose_to(mn, "mn", tl)
        for t in range(11):
            wgb, wgk = wtile()
            k.dma(sp, wgb[:], W["WG"].ap()[t], reads=[("WG", t)], writes=[wgk])
            wub, wuk = wtile()
            k.dma(sp, wub[:], W["WU"].ap()[t], reads=[("WU", t)], writes=[wuk])
            for hc in range(4):
                bg_t, bgk = bank()
                for kc in range(16):
                    k.op(pe, lambda: nc.tensor.matmul(bg_t[:], lhsT=wgb[:, kc, hc * 128:(hc + 1) * 128], rhs=tT[:, kc, :],
                                                      start=(kc == 0), stop=(kc == 15)),
                         reads=[wgk] + tTkeys, writes=[bgk], inc=(kc == 15))
                bu_t, buk = bank()
                for kc in range(16):
                    k.op(pe, lambda: nc.tensor.matmul(bu_t[:], lhsT=wub[:, kc, hc * 128:(hc + 1) * 128], rhs=tT[:, kc, :],
                                                      start=(kc == 0), stop=(kc == 15)),
                         reads=[wuk] + tTkeys, writes=[buk], inc=(kc == 15))
                sgb = sg[st["sgi"] % 2]
                sgk = f"sg{st['sgi'] % 2}"
                st["sgi"] += 1
                k.op(act, lambda: nc.scalar.activation(out=sgb[:], in_=bg_t[:], func=AF.Silu), reads=[bgk], writes=[sgk])
                k.op(dve, lambda: nc.vector.tensor_tensor(out=actT[:, t * 4 + hc, :], in0=bu_t[:], in1=sgb[:],
                                                          op=ALU.mult),
                     reads=[buk, sgk], writes=[("actT", t * 4 + hc)])
        for cb in range(4):
            banks = [bank() for _ in range(4)]
            for kq in range(4):
                wb, wk = wtile()
                k.dma(sp, wb[:, 0:11, :], W["WD"].ap()[cb, kq], reads=[("WD", cb, kq)], writes=[wk])
                for tl in range(4):
                    bk_t, bk = banks[tl]
                    for kc in range(11):
                        kk = kq * 11 + kc
                        k.op(pe, lambda: nc.tensor.matmul(bk_t[:], lhsT=actT[:, kk, tl * 128:(tl + 1) * 128],
                                                          rhs=wb[:, kc, :], start=(kk == 0), stop=(kk == 43)),
                             reads=[wk, ("actT", kk)], writes=[bk], inc=(kc == 10))
            for tl in range(4):
                bk_t, bk = banks[tl]
                hs = h1[:, tl, cb * 512:(cb + 1) * 512]
                k.op(dve, lambda: nc.vector.tensor_tensor(out=hs, in0=bk_t[:], in1=hs, op=ALU.add),
                     reads=[bk, ("h1", tl)], writes=[("h1", tl)])
        k.dma(sp, gbc[:], W["gf"].ap().to_broadcast([128, D]), writes=["gbc"])
        for tl in range(4):
            tt = tb * 4 + tl
            k.op(act, lambda: nc.scalar.activation(out=junk[:], in_=h1[:, tl, :], func=AF.Square,
                                                   scale=1.0 / math.sqrt(D), accum_out=sq[:, 0:1]),
                 reads=[("h1", tl)], writes=["junk", "sq"])
            rsqrt_eps(k, sq, "sq", 1)
            k.op(dve, lambda: nc.vector.scalar_tensor_tensor(out=outt[:], in0=h1[:, tl, :], scalar=sq[:, 1:2], in1=gbc[:],
                                                             op0=ALU.mult, op1=ALU.mult),
                 reads=[("h1", tl), "sq", "gbc"], writes=["outt"])
            k.dma(sp, out_v[tt], outt[:], reads=["outt"], writes=[("out", tt)])
    es.close()


def phase1(k, x, w_in, g1, ident, PS, T):
    nc = k.nc
    pe, act, dve, pool, sp = k.pe, k.act, k.dve, k.pool, k.sp
    es = ExitStack()

    def sb(name, shape, dt):
        return es.enter_context(nc.sbuf_tensor(name, list(shape), dt))

    xt = [sb(f"p1_xt{i}", [128, D], F32) for i in range(2)]
    hn = [sb(f"p1_hn{i}", [128, D], BF16) for i in range(2)]
    junk = sb("p1_junk", [128, D], BF16)
    g1bc = sb("p1_g1bc", [128, D], F32)
    ss = [sb(f"p1_ss{i}", [128, 2], F32) for i in range(2)]
    hnT = sb("p1_hnT", [128, 16, NQ], BF16)
    wt = [sb(f"p1_wt{i}", [128, 16, 512], BF16) for i in range(2)]
    ybuf = [sb(f"p1_yb{i}", [128, NQ], BF16) for i in range(2)]
    ybuf32 = [sb(f"p1_yc{i}", [128, NQ], F32) for i in range(2)]
    xbuf = [sb(f"p1_xb{i}", [128, 512], BF16) for i in range(3)]
    gbuf = [sb(f"p1_gb{i}", [128, 24], F32) for i in range(2)]

    k.dma(sp, g1bc[:], g1.ap().to_broadcast([128, D]), writes=["g1bc"])

    w_v = w_in.ap().rearrange("(kc p) c -> p kc c", p=128)
    x_v = x.ap().rearrange("(t p) f -> t p f", p=128)

    blocks = []
    for b in range(11):
        c0 = b * 512
        subs = []
        if b in (0, 1):
            subs = [("Yq", cc, 128, "QTa", b * 4 + cc // 128) for cc in range(0, 512, 128)]
        elif b in (2, 3):
            subs = [("Y", cc, 128, "KTa", (b - 2) * 4 + cc // 128) for cc in range(0, 512, 128)]
        elif b in (4, 5):
            subs = [("X", 0, 512, "Va", (b - 4) * 512)]
        elif b in (6, 7):
            subs = [("Yq", cc, 128, "QTb", (b - 6) * 4 + cc // 128) for cc in range(0, 512, 128)]
        elif b == 8:
            subs = [("Y32", cc, 128, "KVc", cc // 128) for cc in range(0, 512, 128)]
        elif b == 9:
            subs = [("Y", 0, 128, "KTs", 0), ("Y", 128, 128, "KTs", 1), ("X", 256, 256, "Vs", 0)]
        elif b == 10:
            subs = [("Y", 0, 128, "KTw", 0), ("Y", 128, 128, "KTw", 1), ("X", 256, 256, "Vw", 0)]
        blocks.append((c0, 512, subs))
    blocks.append((5632, 24, [("G", 0, 24, "Gt", 0)]))
    qonly = {0, 1, 6, 7, 11}

    wi = 0
    yi = 0
    y32i = 0
    xi = 0
    gi = 0
    psi = 0
    evi = 0
    for pss in range(2):
        tok0 = pss * NQ
        for tt in range(16):
            xb = xt[tt % 2]
            hb = hn[tt % 2]
            sq = ss[tt % 2]
            kx, kh, ks = f"xt{tt % 2}", f"hn{tt % 2}", f"ss{tt % 2}"
            k.dma(sp, xb[:], x_v[pss * 16 + tt], writes=[kx])
            k.op(act, lambda: nc.scalar.activation(out=junk[:], in_=xb[:], func=AF.Square, scale=1.0 / math.sqrt(D),
                                                   accum_out=sq[:, 0:1]),
                 reads=[kx], writes=["junk", ks])
            rsqrt_eps(k, sq, ks)
            k.op(dve, lambda: nc.vector.scalar_tensor_tensor(out=hb[:], in0=xb[:], scalar=sq[:, 1:2], in1=g1bc[:],
                                                             op0=ALU.mult, op1=ALU.mult),
                 reads=[kx, ks, "g1bc"], writes=[kh])
            for half in range(2):
                bank = PS[psi % 8]
                bk = f"ps{psi % 8}"
                psi += 1
                pv = bank[:].bitcast(BF16)
                for j in range(8):
                    kc = half * 8 + j
                    k.op(pe, lambda: nc.tensor.transpose(out=pv[:, j * 128:(j + 1) * 128],
                                                         in_=hb[:, kc * 128:(kc + 1) * 128], identity=ident[:]),
                         reads=[kh, "ident"], writes=[bk], inc=(j == 7))
                dst = hnT[:, half * 8:half * 8 + 8, tt * 128:(tt + 1) * 128]
                src = pv.rearrange("p (a b) -> p a b", a=8)
                if evi % 2 == 0:
                    k.op(act, lambda: nc.scalar.copy(out=dst, in_=src), reads=[bk], writes=[("hnT", tt)])
                else:
                    k.op(dve, lambda: nc.vector.tensor_copy(out=dst, in_=src), reads=[bk], writes=[("hnT", tt)])
                evi += 1
        for bi, (c0, ncols, subs) in enumerate(blocks):
            if pss == 0 and bi in qonly:
                continue
            wb = wt[wi % 2]
            wk = f"wt{wi % 2}"
            wi += 1
            k.dma(pool, wb[:, :, 0:ncols], w_v[:, :, c0:c0 + ncols], writes=[wk])
            for (kind, cc, n, dname, didx) in subs:
                dst_t = T[dname]
                if kind in ("Y", "Yq", "Y32"):
                    if kind == "Y32":
                        yb = ybuf32[y32i % 2]
                        yk = f"yc{y32i % 2}"
                        y32i += 1
                    else:
                        yb = ybuf[yi % 2]
                        yk = f"yb{yi % 2}"
                        yi += 1
                    for tb in range(4):
                        bank = PS[psi % 8]
                        bk = f"ps{psi % 8}"
                        psi += 1
                        for kc in range(16):
                            k.op(pe, lambda: nc.tensor.matmul(bank[:], lhsT=wb[:, kc, cc:cc + 128],
                                                              rhs=hnT[:, kc, tb * 512:(tb + 1) * 512],
                                                              start=(kc == 0), stop=(kc == 15)),
                                 reads=[wk] + [("hnT", t) for t in range(tb * 4, tb * 4 + 4)], writes=[bk],
                                 inc=(kc == 15))
                        o = yb[:, tb * 512:(tb + 1) * 512]
                        sc = QSCALE if kind == "Yq" else 1.0
                        if evi % 2 == 0:
                            k.op(act, lambda: nc.scalar.mul(out=o, in_=bank[:], mul=sc), reads=[bk], writes=[yk])
                        else:
                            k.op(dve, lambda: nc.vector.tensor_scalar(out=o, in0=bank[:], scalar1=sc, scalar2=None,
                                                                      op0=ALU.mult), reads=[bk], writes=[yk])
                        evi += 1
                    if kind == "Yq":
                        dap = dst_t.ap()[didx]
                    else:
                        dap = dst_t.ap()[didx][:, tok0:tok0 + NQ]
                    k.dma(sp, dap, yb[:], reads=[yk], writes=[(dname, didx, pss)])
                elif kind == "X":
                    for tt in range(16):
                        bank = PS[psi % 8]
                        bk = f"ps{psi % 8}"
                        psi += 1
                        for kc in range(16):
                            k.op(pe, lambda: nc.tensor.matmul(bank[:, 0:n], lhsT=hnT[:, kc, tt * 128:(tt + 1) * 128],
                                                              rhs=wb[:, kc, cc:cc + n],
                                                              start=(kc == 0), stop=(kc == 15)),
                                 reads=[wk, ("hnT", tt)], writes=[bk], inc=(kc == 15))
                        xbf = xbuf[xi % 3]
                        xk = f"xb{xi % 3}"
                        xi += 1
                        if evi % 2 == 0:
                            k.op(act, lambda: nc.scalar.copy(out=xbf[:, 0:n], in_=bank[:, 0:n]), reads=[bk], writes=[xk])
                        else:
                            k.op(dve, lambda: nc.vector.tensor_copy(out=xbf[:, 0:n], in_=bank[:, 0:n]), reads=[bk],
                                 writes=[xk])
                        evi += 1
                        r0 = tok0 + tt * 128
                        k.dma(sp, dst_t.ap()[r0:r0 + 128, didx:didx + n], xbf[:, 0:n], reads=[xk],
                              writes=[(dname, pss, tt, didx)])
                elif kind == "G":
                    for tt in range(16):
                        bank = PS[psi % 8]
                        bk = f"ps{psi % 8}"
                        psi += 1
                        for kc in range(16):
                            k.op(pe, lambda: nc.tensor.matmul(bank[:, 0:n], lhsT=hnT[:, kc, tt * 128:(tt + 1) * 128],
                                                              rhs=wb[:, kc, cc:cc + n],
                                                              start=(kc == 0), stop=(kc == 15)),
                                 reads=[wk, ("hnT", tt)], writes=[bk], inc=(kc == 15))
                        gb = gbuf[gi % 2]
                        gk = f"gb{gi % 2}"
                        gi += 1
                        k.op(act, lambda: nc.scalar.activation(out=gb[:], in_=bank[:, 0:n], func=AF.Sigmoid),
                             reads=[bk], writes=[gk])
                        k.dma(sp, dst_t.ap()[tt * 128:(tt + 1) * 128, :], gb[:], reads=[gk], writes=[(dname, tt)])
    es.close()


def alibi_slopes16():
    return np.exp2(-8.0 * np.arange(1, 17, dtype=np.float32) / 16.0).astype(np.float32)


def make_tables(half):
    sl = alibi_slopes16()
    sa = sl[0::2]
    kk = np.arange(128, dtype=np.float32)[:, None]
    qq = np.arange(128, dtype=np.float32)[None, :]
    biasA = np.zeros((8, 3, 2, 128, 256), np.float32)
    for h in range(8):
        for di, dil in enumerate((1, 4, 16)):
            relp = qq - kk + 128.0
            prev = np.where(kk >= qq, -sa[h] * dil * relp, NEG)
            reld = qq - kk
            diag = np.where(reld >= 0, -sa[h] * dil * reld, NEG)
            biasA[h, di, 0, :, 0:128] = prev
            biasA[h, di, 0, :, 128:256] = diag
            biasA[h, di, 1, :, 0:128] = prev if half == 1 else NEG
            biasA[h, di, 1, :, 128:256] = diag
    sbv = sl[1::2].reshape(2, 4)
    Ltab = np.zeros((69, 32, 128), np.float32)
    kl = np.arange(128)
    for j in range(32):
        Ltab[2 * j + kl // 64, j, kl] = 1.0
        Ltab[64, j, :] = 1.0
        Ltab[65, j, :] = 1.0
        Ltab[66, j, :] = j
        Ltab[67, j, :] = kl
        Ltab[68, j, :] = NEG if (half == 0 and j < 16) else 0.0
    Lc = np.zeros((4, 2, 128), np.float32)
    for jc in range(2):
        Lc[0, jc] = 1.0
        Lc[1, jc] = 1.0
        Lc[2, jc] = 128 * jc + kl
        Lc[3, jc] = 1.0
    kq = np.arange(128)
    le = np.where(kq[:, None] <= kq[None, :], 0.0, NEG).astype(np.float32)
    gt = np.where(kq[:, None] > kq[None, :], 0.0, NEG).astype(np.float32)
    maskLE = np.tile(le, (1, 4))
    maskGT = np.tile(gt, (1, 4))
    addtab = np.zeros((128, 16, 64), np.float32)
    for i in range(16):
        t_real = half * 2048 + 128 * i + kq
        cur = t_real // 64
        nr = np.arange(64)[None, :] - 32 * (1 - half)
        valid = (nr >= 0) & (nr <= cur[:, None])
        forced = (nr == 0) | (valid & (nr > cur[:, None] - 2))
        addtab[:, i, :] = np.where(valid, np.where(forced, 1000.0, 0.0), -1e4)
    c = (128 * np.arange(2)[None, :, None] + kl[:, None, None])
    n = np.arange(64)[None, None, :]
    ovl = (((16 * c) < (64 * n + 64)) & ((16 * c + 32) > (64 * n)) & (c <= 254)).astype(np.float32)
    Rstat = np.zeros((2, 16, 5, 4, 128), np.float32)
    Rc = np.zeros((2, 16, 4, 4, 128), np.float32)
    for g in range(2):
        for hq in range(4):
            sh = sbv[g, hq]
            for i in range(16):
                Rstat[g, i, 0, hq, :] = -sh * 128.0 * (16 + i)
                Rstat[g, i, 1, hq, :] = -sh * kq
                Rstat[g, i, 2, hq, :] = sh * 128.0
                Rstat[g, i, 3, hq, :] = sh
                Rstat[g, i, 4, hq, :] = 1.0
                Rc[g, i, 0, hq, :] = -sh * 128.0 * (16 + i)
                Rc[g, i, 1, hq, :] = -sh * kq
                Rc[g, i, 2, hq, :] = 16.0 * sh
                Rc[g, i, 3, hq, :] = 31.0 * sh
    cmpmask = np.zeros((16, 128, 2, 4, 128), np.float32)
    for i in range(16):
        t_win = 2048 + 128 * i + kq
        cc = c[:, :, 0]
        ok = (cc[:, :, None] <= 254) & ((16 * cc[:, :, None] + 31) <= t_win[None, None, :])
        if half == 0:
            ok = ok & (cc[:, :, None] >= 128)
        cmpmask[i] = np.where(ok, 0.0, NEG)[:, :, None, :]
    bf = lambda a: np.ascontiguousarray(a).astype(NBF)
    return {"biasA": biasA, "Ltab": bf(Ltab), "Lc": bf(Lc), "maskLE": bf(maskLE), "maskGT": bf(maskGT),
            "addtab": addtab, "ovl": bf(ovl), "Rstat": bf(Rstat.reshape(2, 16, 5, 512)),
            "Rc": bf(Rc.reshape(2, 16, 4, 512)), "cmpmask": bf(cmpmask.reshape(16, 128, 2, 512))}


def make_in_maps(inputs):
    xs = np.asarray(inputs["x"], dtype=np.float32)
    B = xs.shape[0]
    common = {
        "w_in": np.ascontiguousarray(np.asarray(inputs["w_in"], np.float32)[0]),
        "g1": np.ascontiguousarray(np.asarray(inputs["norm1_g"], np.float32)[0].reshape(1, D)),
        "ident": np.eye(128, dtype=np.float32).astype(NBF),
        "w_out": np.ascontiguousarray(np.asarray(inputs["w_out"], np.float32)[0]),
        "w_gate": np.ascontiguousarray(np.asarray(inputs["w_gate"], np.float32)[0]),
        "w_up": np.ascontiguousarray(np.asarray(inputs["w_up"], np.float32)[0]),
        "w_down": np.ascontiguousarray(np.asarray(inputs["w_down"], np.float32)[0]),
        "gab": np.concatenate([np.asarray(inputs["grp_norm_a"], np.float32)[0],
                               np.asarray(inputs["grp_norm_b"], np.float32)[0]]).reshape(1, D),
        "g2": np.ascontiguousarray(np.asarray(inputs["norm2_g"], np.float32)[0].reshape(1, D)),
        "gf": np.ascontiguousarray(np.asarray(inputs["final_g"], np.float32).reshape(1, D)),
        "pe_k": np.ascontiguousarray(np.asarray(inputs["cmp_pe_k"], np.float32)[0]),
        "pe_v": np.ascontiguousarray(np.asarray(inputs["cmp_pe_v"], np.float32)[0]),
        "w1_k": np.ascontiguousarray(np.asarray(inputs["cmp_w1_k"], np.float32)[0]),
        "w1_v": np.ascontiguousarray(np.asarray(inputs["cmp_w1_v"], np.float32)[0]),
        "w2_k": np.ascontiguousarray(np.asarray(inputs["cmp_w2_k"], np.float32)[0]),
        "w2_v": np.ascontiguousarray(np.asarray(inputs["cmp_w2_v"], np.float32)[0]),
    }
    maps = []
    for c in range(8):
        b, half = c // 2, c % 2
        if half == 0:
            xc = np.concatenate([np.zeros((NQ, D), np.float32), xs[b, :NQ]], axis=0)
        else:
            xc = xs[b]
        m = dict(common)
        m.update(make_tables(half))
        m["x"] = np.ascontiguousarray(xc)
        maps.append(m)
    return maps


def kernel(**inputs):
    k = build()
    maps = make_in_maps(inputs)
    res = run_bass_kernel_spmd(k.nc, maps, core_ids=list(range(8)))
    outs = [r["out"] for r in res.results]
    full = np.zeros((4, S, D), np.float32)
    for c in range(8):
        b, half = c // 2, c % 2
        full[b, half * NQ:(half + 1) * NQ] = outs[c]
    return full
```

```python
import math
from contextlib import ExitStack

import numpy as np
import ml_dtypes

import concourse.bass as bass
import concourse.mybir as mybir
from concourse.bass_utils import run_bass_kernel_spmd

F32 = mybir.dt.float32
BF16 = mybir.dt.bfloat16
AF = mybir.ActivationFunctionType
ALU = mybir.AluOpType
AX = mybir.AxisListType

D = 2048
S = 4096
NQ = 2048
HD = 128
PIN = 5656
FFN = 5632
EPS = 1e-6
NEG = -30000.0
QSCALE = 1.0 / math.sqrt(128.0)
NBF = ml_dtypes.bfloat16


class Eng:
    EPOCH = 8000

    def __init__(self, k, name, h, track_self=True):
        self.k, self.name, self.h = k, name, h
        self.sems = []
        self.count = 0
        self.seen = {}
        self.track_self = track_self
        self._new_epoch()

    def _new_epoch(self):
        sem = self.k.es.enter_context(self.k.nc.semaphore(f"{self.name}_e{len(self.sems)}"))
        self.sems.append(sem)
        self.count = 0

    def next_event(self):
        if self.count >= self.EPOCH:
            self._new_epoch()
        return (self.sems[-1], self.count + 1)

    def wait(self, sem, val):
        if self.seen.get(id(sem), 0) >= val:
            return
        self.h.wait_ge(sem, val)
        self.seen[id(sem)] = val


class DmaSem:
    def __init__(self, sem):
        self.sem = sem
        self.count = 0


class K:
    def __init__(self):
        self.nc = bass.Bass("TRN2", target_bir_lowering=False)
        self.es = ExitStack()
        nc = self.nc
        self.pe = Eng(self, "pe", nc.tensor, track_self=False)
        self.act = Eng(self, "act", nc.scalar)
        self.dve = Eng(self, "dve", nc.vector)
        self.pool = Eng(self, "pool", nc.gpsimd)
        self.sp = Eng(self, "sp", nc.sync)
        self.lastw = {}
        self.readers = {}
        self.dsems = [DmaSem(self.es.enter_context(nc.semaphore(f"dma{i}"))) for i in range(32)]
        self.drr = 0
        self.psems = [DmaSem(self.es.enter_context(nc.semaphore(f"sdma{i}"))) for i in range(6)]
        self.prr = 0
        self.csems = [DmaSem(self.es.enter_context(nc.semaphore(f"cnv{i}"))) for i in range(4)]
        self.crr = 0
        self.pool_hist = []
        self.dram = {}
        self.nps = 0

    def sb(self, name, shape, dt):
        return self.es.enter_context(self.nc.sbuf_tensor(name, list(shape), dt))

    def ps(self, name):
        return self.es.enter_context(self.nc.psum_tensor(name, [128, 512], F32))

    def dt_in(self, name, shape, dt):
        t = self.nc.dram_tensor(name, list(shape), dt, kind="ExternalInput")
        self.dram[name] = t
        return t

    def dt_out(self, name, shape, dt):
        t = self.nc.dram_tensor(name, list(shape), dt, kind="ExternalOutput")
        self.dram[name] = t
        return t

    def dt_tmp(self, name, shape, dt, debug=False, shared=False):
        t = self.nc.dram_tensor(name, list(shape), dt, kind="Internal" if shared else "ExternalOutput")
        self.dram[name] = t
        return t

    def _deps(self, reads, writes):
        deps = {}

        def add(ev):
            if ev is None:
                return
            s, v = ev
            if deps.get(id(s), (None, 0))[1] < v:
                deps[id(s)] = (s, v)

        for b in reads:
            add(self.lastw.get(b))
        for b in writes:
            add(self.lastw.get(b))
            for ev in self.readers.get(b, {}).values():
                add(ev)
        return deps

    def _mark(self, ev, reads, writes):
        s, v = ev
        for b in reads:
            r = self.readers.setdefault(b, {})
            if r.get(id(s), (None, 0))[1] < v:
                r[id(s)] = ev
        for b in writes:
            self.lastw[b] = ev
            self.readers[b] = {}

    def op(self, eng, fn, reads=(), writes=(), inc=True):
        ev = eng.next_event()
        own = {id(s) for s in eng.sems}
        for sid, (s, v) in self._deps(reads, writes).items():
            if sid in own and not eng.track_self:
                continue
            eng.wait(s, v)
        ins = fn()
        if inc:
            ins.then_inc(ev[0], 1)
            eng.count += 1
        self._mark(ev, reads, writes)
        return ins

    def dma(self, q, out, in_, reads=(), writes=(), dram_only=False, **kw):
        if dram_only:
            ds = self.csems[self.crr]
            self.crr = (self.crr + 1) % len(self.csems)
        elif q is self.pool:
            ds = self.psems[self.prr]
            self.prr = (self.prr + 1) % len(self.psems)
        else:
            ds = self.dsems[self.drr]
            self.drr = (self.drr + 1) % len(self.dsems)
        q.wait(ds.sem, ds.count * 16)
        if q is self.pool:
            if len(self.pool_hist) >= 3:
                pe_ = self.pool_hist[-3]
                q.wait(pe_[0], pe_[1])
        for sid, (s, v) in self._deps(reads, writes).items():
            q.wait(s, v)
        q.h.dma_start(out=out, in_=in_, **kw).then_inc(ds.sem, 16)
        ds.count += 1
        ev = (ds.sem, ds.count * 16)
        if q is self.pool:
            self.pool_hist.append(ev)
        self._mark(ev, reads, writes)
        return ev

    def barrier(self):
        engs = [self.pe, self.act, self.dve, self.pool, self.sp]
        evs = [(e.sems[-1], e.count) for e in engs if e.count > 0]
        evs += [(d.sem, d.count * 16) for d in self.dsems + self.psems if d.count > 0]
        for e in engs:
            for (sem, v) in evs:
                e.wait(sem, v)

    def finish(self, keys):
        for b in keys:
            ev = self.lastw.get(b)
            if ev is not None:
                self.sp.wait(ev[0], ev[1])


def build(debug=False, phases=("p1", "p2", "p3"), mix_input=False):
    k = K()
    nc = k.nc
    pe, act, dve, pool, sp = k.pe, k.act, k.dve, k.pool, k.sp

    x = k.dt_in("x", [S, D], F32)
    w_in = k.dt_in("w_in", [D, PIN], F32)
    g1 = k.dt_in("g1", [1, D], F32)
    ident_d = k.dt_in("ident", [128, 128], BF16)
    biasA_d = k.dt_in("biasA", [8, 3, 2, 128, 256], F32)
    C = {}
    for nm, shp, dt in (("Ltab", [69, 32, 128], BF16), ("Lc", [4, 2, 128], BF16), ("maskLE", [128, 512], BF16),
                        ("maskGT", [128, 512], BF16), ("addtab", [128, 16, 64], F32), ("ovl", [128, 2, 64], BF16),
                        ("Rstat", [2, 16, 5, 512], BF16), ("Rc", [2, 16, 4, 512], BF16),
                        ("cmpmask", [16, 128, 2, 512], BF16),
                        ("pe_k", [32, 128], F32), ("pe_v", [32, 128], F32), ("w1_k", [4096, 256], F32),
                        ("w1_v", [4096, 256], F32), ("w2_k", [256, 128], F32), ("w2_v", [256, 128], F32)):
        C[nm] = k.dt_in(nm, shp, dt)
    out = k.dt_out("out", [NQ, D], F32)

    KTa = k.dt_tmp("KTa", [8, 128, S], BF16, debug)
    QTa = k.dt_tmp("QTa", [8, 128, NQ], BF16, debug)
    Va = k.dt_tmp("Va", [S, 1024], BF16, debug)
    QTb = k.dt_tmp("QTb", [8, 128, NQ], BF16, debug)
    KVc = k.dt_tmp("KVc", [4, 128, S], F32, debug)
    KTs = k.dt_tmp("KTs", [2, 128, S], BF16, debug)
    Vs = k.dt_tmp("Vs", [S, 256], BF16, debug)
    KTw = k.dt_tmp("KTw", [2, 128, S], BF16, debug)
    Vw = k.dt_tmp("Vw", [S, 256], BF16, debug)
    Gt = k.dt_tmp("Gt", [NQ, 24], F32, debug)

    W = {}
    for nm, shp in (("w_out", [D, D]), ("w_gate", [D, FFN]), ("w_up", [D, FFN]), ("w_down", [FFN, D]),
                    ("gab", [1, D]), ("g2", [1, D]), ("gf", [1, D])):
        W[nm] = k.dt_in(nm, shp, F32)
    W["W1c"] = k.dt_tmp("W1c", [2, 128, 32, 256], BF16, shared=True)
    W["W2c"] = k.dt_tmp("W2c", [2, 128, 2, 128], BF16, shared=True)
    W["WO"] = k.dt_tmp("WO", [4, 128, 16, 512], BF16, shared=True)
    W["WG"] = k.dt_tmp("WG", [11, 128, 16, 512], BF16, shared=True)
    W["WU"] = k.dt_tmp("WU", [11, 128, 16, 512], BF16, shared=True)
    W["WD"] = k.dt_tmp("WD", [4, 4, 128, 11, 512], BF16, shared=True)
    if mix_input:
        MIX = k.dt_in("MIX", [NQ, D], BF16)
    else:
        MIX = k.dt_tmp("MIX", [NQ, D], BF16, debug)

    PS = [k.ps(f"ps{i}") for i in range(8)]

    ident = k.sb("ident_sb", [128, 128], BF16)
    k.dma(sp, ident[:], ident_d.ap(), writes=["ident"])

    if "p1" in phases:
        phase1(k, x, w_in, g1, ident, PS, dict(KTa=KTa, QTa=QTa, Va=Va, QTb=QTb, KVc=KVc, KTs=KTs, Vs=Vs,
                                                 KTw=KTw, Vw=Vw, Gt=Gt))
    st = dict(psi=0)
    k.barrier()
    phase0(k, W, C, with_dense=("p3" in phases))
    if "p2a" in phases or "p2" in phases:
        phase2a(k, dict(KTa=KTa, QTa=QTa, Va=Va), ident, PS, MIX, biasA_d, st)
    k.barrier()
    if "p2b" in phases or "p2" in phases:
        DBG = None
        if debug:
            DBG = dict(kc=k.dt_out("DBGkc", [2, 128, 256], BF16), vc=k.dt_out("DBGvc", [2, 128, 2, 193], BF16),
                       o=k.dt_out("DBGo", [3, NQ, 1024], F32), score=k.dt_out("DBGscore", [2, 16, 128, 64], F32),
                       t1=k.dt_out("DBGt1", [2, 16, 128, 64], F32))
        C2 = dict(C)
        C2["W1c"], C2["W2c"] = W["W1c"], W["W2c"]
        phase2b(k, dict(KVc=KVc, KTs=KTs, Vs=Vs, KTw=KTw, Vw=Vw, QTb=QTb, Gt=Gt), ident, PS, MIX, C2, st, DBG)
    k.barrier()
    if "p3" in phases:
        phase3(k, x, W, ident, PS, MIX, out)

    k.finish(list(k.lastw.keys()))
    return k


def rsqrt_eps(k, sq, key=None, n=1):
    nc = k.nc
    ks = key
    k.op(k.dve, lambda: nc.vector.tensor_scalar(out=sq[:, n:2 * n], in0=sq[:, 0:n], scalar1=EPS, scalar2=None,
                                                op0=ALU.add), reads=[ks], writes=[ks])
    k.op(k.dve, lambda: nc.vector.reciprocal(out=sq[:, n:2 * n], in_=sq[:, n:2 * n]), reads=[ks], writes=[ks])
    k.op(k.act, lambda: nc.scalar.activation(out=sq[:, n:2 * n], in_=sq[:, n:2 * n], func=AF.Sqrt), reads=[ks],
         writes=[ks])


def phase0(k, W, C, with_dense=True):
    pool = k.pool
    for kv, nm in enumerate(("k", "v")):
        k.dma(pool, W["W1c"].ap()[kv], C[f"w1_{nm}"].ap().rearrange("(p d) c -> d p c", d=128), writes=[("W1c", kv)], dram_only=True)
        k.dma(pool, W["W2c"].ap()[kv], C[f"w2_{nm}"].ap().rearrange("(hc p) c -> p hc c", p=128), writes=[("W2c", kv)], dram_only=True)
    if not with_dense:
        return
    wo = W["w_out"].ap().rearrange("(kc p) (t c) -> t p kc c", p=128, c=512)
    for t in range(4):
        k.dma(pool, W["WO"].ap()[t], wo[t], writes=[("WO", t)], dram_only=True)
    wg = W["w_gate"].ap().rearrange("(kc p) (t c) -> t p kc c", p=128, c=512)
    wu = W["w_up"].ap().rearrange("(kc p) (t c) -> t p kc c", p=128, c=512)
    for t in range(11):
        k.dma(pool, W["WG"].ap()[t], wg[t], writes=[("WG", t)], dram_only=True)
        k.dma(pool, W["WU"].ap()[t], wu[t], writes=[("WU", t)], dram_only=True)
    wd = W["w_down"].ap().rearrange("(kq kc p) (cb c) -> cb kq p kc c", p=128, kc=11, c=512)
    for cb in range(4):
        for kq in range(4):
            k.dma(pool, W["WD"].ap()[cb, kq], wd[cb, kq], writes=[("WD", cb, kq)], dram_only=True)


def phase2a(k, T, ident, PS, MIX, biasA_d, st):
    nc = k.nc
    pe, act, dve, pool, sp = k.pe, k.act, k.dve, k.pool, k.sp
    es = ExitStack()

    def sb(name, shape, dt):
        return es.enter_context(nc.sbuf_tensor(name, list(shape), dt))

    KT = [sb(f"a_KT{i}", [128, S], BF16) for i in range(2)]
    QT = [sb(f"a_QT{i}", [128, NQ], BF16) for i in range(2)]
    KT4 = sb("a_KT4", [128, 4, 1024], BF16)
    KT16 = sb("a_KT16", [128, 16, 256], BF16)
    QT4 = sb("a_QT4", [128, 4, 512], BF16)
    QT16 = sb("a_QT16", [128, 16, 128], BF16)
    V1 = [sb(f"a_V1{i}", [128, 32, 128], BF16) for i in range(2)]
    V4 = [sb(f"a_V4{i}", [128, 8, 4, 128], BF16) for i in range(2)]
    V16 = [sb(f"a_V16{i}", [128, 2, 16, 128], BF16) for i in range(2)]
    bias = [sb(f"a_bias{i}", [128, 3, 2, 256], F32) for i in range(2)]
    acc2 = sb("a_acc2", [128, 2, NQ], F32)
    sT = [sb(f"a_sT{i}", [128, 256], F32) for i in range(3)]
    pT = [sb(f"a_pT{i}", [128, 256], BF16) for i in range(3)]
    ones = sb("a_ones", [128, 128], BF16)
    rec = sb("a_rec", [128, NQ], F32)
    oT = sb("a_oT", [128, NQ], BF16)
    ob = sb("a_ob", [128, 16, 128], BF16)
    k.op(dve, lambda: nc.vector.memset(ones[:], 1.0), writes=["a_ones"])

    def bank():
        i = st["psi"] % 8
        st["psi"] += 1
        return PS[i], f"ps{i}"

    def issue_loads(h):
        b = h % 2
        kK, kQ, kV1, kV4, kV16, kB = f"aKT{b}", f"aQT{b}", f"aV1{b}", f"aV4{b}", f"aV16{b}", f"abias{b}"
        k.dma(sp, KT[b][:], T["KTa"].ap()[h], reads=[("KTa", h, 0), ("KTa", h, 1)], writes=[kK])
        k.dma(sp, QT[b][:], T["QTa"].ap()[h], reads=[("QTa", h, 1)], writes=[kQ])
        vsrc = T["Va"].ap()[:, h * 128:(h + 1) * 128]
        vreads = [("Va", p_, t_, (h // 4) * 512) for p_ in range(2) for t_ in range(16)]
        k.dma(sp, V1[b][:], vsrc.rearrange("(m kk) c -> kk m c", kk=128), reads=vreads, writes=[kV1])
        v4s = vsrc.rearrange("(m kk r) c -> r kk m c", kk=128, r=4)
        for r in range(4):
            k.dma(sp, V4[b][:, :, r, :], v4s[r], reads=vreads, writes=[kV4])
        v16s = vsrc.rearrange("(m kk r) c -> m kk r c", kk=128, r=16)
        for m in range(2):
            k.dma(sp, V16[b][:, m, :, :], v16s[m], reads=vreads, writes=[kV16])
        k.dma(sp, bias[b][:], biasA_d.ap()[h].rearrange("d v kk c -> kk d v c"), writes=[kB])

    ui = 0
    issue_loads(0)
    for h in range(8):
        b = h % 2
        kK, kQ, kV1, kV4, kV16, kB = f"aKT{b}", f"aQT{b}", f"aV1{b}", f"aV4{b}", f"aV16{b}", f"abias{b}"
        if h + 1 < 8:
            issue_loads(h + 1)
        k.op(act, lambda: nc.scalar.copy(out=KT4[:], in_=KT[b][:].rearrange("p (a r) -> p r a", r=4)),
             reads=[kK], writes=["aKT4"])
        k.op(act, lambda: nc.scalar.copy(out=KT16[:], in_=KT[b][:].rearrange("p (a r) -> p r a", r=16)),
             reads=[kK], writes=["aKT16"])
        k.op(act, lambda: nc.scalar.copy(out=QT4[:], in_=QT[b][:].rearrange("p (a r) -> p r a", r=4)),
             reads=[kQ], writes=["aQT4"])
        k.op(act, lambda: nc.scalar.copy(out=QT16[:], in_=QT[b][:].rearrange("p (a r) -> p r a", r=16)),
             reads=[kQ], writes=["aQT16"])
        units = []
        for n in range(16, 32):
            qt = n - 16
            units.append((0, n == 16, KT[b][:, (n - 1) * 128:n * 128], KT[b][:, n * 128:(n + 1) * 128],
                          QT[b][:, qt * 128:(qt + 1) * 128], V1[b][:, n - 1, :], V1[b][:, n, :],
                          slice(qt * 128, (qt + 1) * 128), [kK, kQ, kV1], True))
        for r in range(4):
            for n in range(4, 8):
                units.append((1, n == 4, KT4[:, r, (n - 1) * 128:n * 128], KT4[:, r, n * 128:(n + 1) * 128],
                              QT4[:, r, (n - 4) * 128:(n - 3) * 128], V4[b][:, n - 1, r, :], V4[b][:, n, r, :],
                              slice(512 * (n - 4) + r, 512 * (n - 4) + 512, 4), ["aKT4", "aQT4", kV4], False))
        for r in range(16):
            units.append((2, True, KT16[:, r, 0:128], KT16[:, r, 128:256], QT16[:, r, :], V16[b][:, 0, r, :],
                          V16[b][:, 1, r, :], slice(r, NQ, 16), ["aKT16", "aQT16", kV16], False))
        nu = len(units)
        live = {}
        for t in range(nu + 3):
            if t < nu:
                (di, first, Kp, Kd, Qn, Vp, Vd, osl, rk, is_copy) = units[t]
                b1, b1k = bank()
                k.op(pe, lambda: nc.tensor.matmul(b1[:, 0:128], lhsT=Kp, rhs=Qn, start=True, stop=True), reads=rk,
                     writes=[b1k], inc=False)
                k.op(pe, lambda: nc.tensor.matmul(b1[:, 128:256], lhsT=Kd, rhs=Qn, start=True, stop=True), reads=rk,
                     writes=[b1k])
                live[t] = dict(b1=b1, b1k=b1k, ui=ui)
                ui += 1
            u = t - 1
            if 0 <= u < nu:
                (di, first, Kp, Kd, Qn, Vp, Vd, osl, rk, is_copy) = units[u]
                L = live[u]
                sTb, sk = sT[L["ui"] % 3], f"asT{L['ui'] % 3}"
                pTb, pk = pT[L["ui"] % 3], f"apT{L['ui'] % 3}"
                b1, b1k = L["b1"], L["b1k"]
                k.op(dve, lambda: nc.vector.tensor_tensor(out=sTb[:], in0=b1[:, 0:256],
                                                          in1=bias[b][:, di, 1 if first else 0, :], op=ALU.add),
                     reads=[b1k, kB], writes=[sk])
                k.op(act, lambda: nc.scalar.activation(out=pTb[:], in_=sTb[:], func=AF.Exp), reads=[sk], writes=[pk])
            u = t - 2
            if 0 <= u < nu:
                (di, first, Kp, Kd, Qn, Vp, Vd, osl, rk, is_copy) = units[u]
                L = live[u]
                pTb, pk = pT[L["ui"] % 3], f"apT{L['ui'] % 3}"
                b2, b2k = bank()
                L["b2"], L["b2k"] = b2, b2k
                k.op(pe, lambda: nc.tensor.matmul(b2[:, 0:128], lhsT=Vp, rhs=pTb[:, 0:128], start=True, stop=False),
                     reads=rk + [pk], writes=[b2k], inc=False)
                k.op(pe, lambda: nc.tensor.matmul(b2[:, 0:128], lhsT=Vd, rhs=pTb[:, 128:256], start=False, stop=True),
                     reads=rk + [pk], writes=[b2k], inc=False)
                k.op(pe, lambda: nc.tensor.matmul(b2[:, 128:256], lhsT=ones[:], rhs=pTb[:, 0:128], start=True,
                                                  stop=False), reads=["a_ones", pk], writes=[b2k], inc=False)
                k.op(pe, lambda: nc.tensor.matmul(b2[:, 128:256], lhsT=ones[:], rhs=pTb[:, 128:256], start=False,
                                                  stop=True), reads=["a_ones", pk], writes=[b2k])
            u = t - 3
            if 0 <= u < nu:
                (di, first, Kp, Kd, Qn, Vp, Vd, osl, rk, is_copy) = units[u]
                L = live.pop(u)
                b2, b2k = L["b2"], L["b2k"]
                src = b2[:, 0:256].rearrange("p (a q) -> p a q", a=2)
                dst = acc2[:, :, osl]
                if is_copy:
                    k.op(dve, lambda: nc.vector.tensor_copy(out=dst, in_=src), reads=[b2k], writes=["a_acc2"])
                else:
                    k.op(dve, lambda: nc.vector.tensor_tensor(out=dst, in0=src, in1=dst, op=ALU.add),
                         reads=[b2k, "a_acc2"], writes=["a_acc2"])
        k.op(dve, lambda: nc.vector.reciprocal(out=rec[:], in_=acc2[:, 1, :]), reads=["a_acc2"], writes=["a_rec"])
        k.op(dve, lambda: nc.vector.tensor_tensor(out=oT[:], in0=acc2[:, 0, :], in1=rec[:], op=ALU.mult),
             reads=["a_acc2", "a_rec"], writes=["a_oT"])
        for half in range(2):
            bt, btk = bank()
            pv = bt[:].bitcast(BF16)
            for j in range(8):
                t = half * 8 + j
                k.op(pe, lambda: nc.tensor.transpose(out=pv[:, j * 128:(j + 1) * 128], in_=oT[:, t * 128:(t + 1) * 128],
                                                     identity=ident[:]),
                     reads=["a_oT", "ident"], writes=[btk], inc=(j == 7))
            k.op(act, lambda: nc.scalar.copy(out=ob[:, half * 8:half * 8 + 8, :],
                                             in_=pv.rearrange("p (a b) -> p a b", a=8)), reads=[btk], writes=["a_ob"])
        k.dma(sp, MIX.ap()[:, h * 128:(h + 1) * 128].rearrange("(t p) c -> p t c", p=128), ob[:], reads=["a_ob"],
              writes=[("MIXa", h)])
    es.close()


def phase2b(k, T, ident, PS, MIX, C, st, DBG=None):
    nc = k.nc
    pe, act, dve, pool, sp = k.pe, k.act, k.dve, k.pool, k.sp
    es = ExitStack()

    def sb(name, shape, dt):
        return es.enter_context(nc.sbuf_tensor(name, list(shape), dt))

    def bank():
        i = st["psi"] % 8
        st["psi"] += 1
        return PS[i], f"ps{i}"

    identf = sb("b_identf", [128, 128], F32)
    k.op(dve, lambda: nc.vector.tensor_copy(out=identf[:], in_=ident[:]), reads=["ident"], writes=["identf"])
    Ltab = sb("b_Ltab", [69, 32, 128], BF16)
    k.dma(sp, Ltab[:], C["Ltab"].ap(), writes=["Ltab"])
    Lc = sb("b_Lc", [4, 2, 128], BF16)
    k.dma(sp, Lc[:], C["Lc"].ap(), writes=["Lc"])
    maskLE = sb("b_maskLE", [128, 512], BF16)
    k.dma(sp, maskLE[:], C["maskLE"].ap(), writes=["maskLE"])
    maskGT = sb("b_maskGT", [128, 512], BF16)
    k.dma(sp, maskGT[:], C["maskGT"].ap(), writes=["maskGT"])
    addtab = sb("b_addtab", [128, 16, 64], F32)
    k.dma(sp, addtab[:], C["addtab"].ap(), writes=["addtab"])
    gsb = sb("b_gsb", [128, 16, 24], F32)
    k.dma(sp, gsb[:], T["Gt"].ap().rearrange("(i q) c -> q i c", q=128), reads=[("Gt", t) for t in range(16)],
          writes=["gsb"])
    kcT = [sb(f"b_kcT{g}", [128, 256], BF16) for g in range(2)]
    vcaug = [sb(f"b_vcaug{g}", [128, 2, 193], BF16) for g in range(2)]
    for g in range(2):
        k.op(dve, lambda: nc.vector.memset(vcaug[g][:, :, 128:129], 1.0), writes=[f"vcaug{g}"])
        k.dma(sp, vcaug[g][:, :, 129:193], C["ovl"].ap(), writes=[f"vcaug{g}"])

    es2 = ExitStack()

    def sb2(name, shape, dt):
        return es2.enter_context(nc.sbuf_tensor(name, list(shape), dt))

    kvT = sb2("c_kvT", [128, S], F32)
    pe_sb = sb2("c_pe", [32, 128], F32)
    peT = sb2("c_peT", [128, 32], F32)
    blocksT = sb2("c_blk", [128, 32, 255], BF16)
    W1sb = sb2("c_W1", [128, 32, 256], BF16)
    W2sb = sb2("c_W2", [128, 2, 128], BF16)
    hidT = sb2("c_hid", [128, 2, 256], BF16)
    k.op(dve, lambda: nc.vector.memset(hidT[:], 0.0), writes=["hidT"])
    for kv in range(2):
        nm = "k" if kv == 0 else "v"
        k.dma(sp, pe_sb[:], C[f"pe_{nm}"].ap(), writes=["pe_sb"])
        k.dma(sp, W1sb[:], C["W1c"].ap()[kv], reads=[("W1c", kv)], writes=["W1sb"])
        k.dma(sp, W2sb[:], C["W2c"].ap()[kv], reads=[("W2c", kv)], writes=["W2sb"])
        bt, btk = bank()
        k.op(pe, lambda: nc.tensor.transpose(out=bt[:, 0:32], in_=pe_sb[:, :], identity=identf[0:32, 0:32]),
             reads=["pe_sb", "identf"], writes=[btk])
        k.op(dve, lambda: nc.vector.tensor_copy(out=peT[:], in_=bt[:, 0:32]), reads=[btk], writes=["peT"])
        for g in range(2):
            idx = kv * 2 + g
            k.dma(sp, kvT[:], T["KVc"].ap()[idx], reads=[("KVc", idx, 0), ("KVc", idx, 1)], writes=["kvT"])
            for p in range(32):
                k.op(dve, lambda: nc.vector.tensor_scalar(out=blocksT[:, p, :], in0=kvT[:, p:p + 16 * 254 + 1:16],
                                                          scalar1=peT[:, p:p + 1], scalar2=None, op0=ALU.add),
                     reads=["kvT", "peT"], writes=["blocksT"])
            for hc in range(2):
                bh, bhk = bank()
                for p in range(32):
                    k.op(pe, lambda: nc.tensor.matmul(bh[:, 0:255], lhsT=W1sb[:, p, hc * 128:(hc + 1) * 128],
                                                      rhs=blocksT[:, p, :], start=(p == 0), stop=(p == 31)),
                         reads=["W1sb", "blocksT"], writes=[bhk], inc=(p == 31))
                k.op(act, lambda: nc.scalar.activation(out=hidT[:, hc, 0:255], in_=bh[:, 0:255], func=AF.Silu),
                     reads=[bhk], writes=["hidT"])
            if kv == 0:
                bo, bok = bank()
                for hc in range(2):
                    k.op(pe, lambda: nc.tensor.matmul(bo[:, 0:256], lhsT=W2sb[:, hc, :], rhs=hidT[:, hc, :],
                                                      start=(hc == 0), stop=(hc == 1)),
                         reads=["W2sb", "hidT"], writes=[bok], inc=(hc == 1))
                k.op(dve, lambda: nc.vector.tensor_copy(out=kcT[g][:], in_=bo[:, 0:256]), reads=[bok], writes=[f"kcT{g}"])
            else:
                for jc in range(2):
                    bo, bok = bank()
                    for hc in range(2):
                        k.op(pe, lambda: nc.tensor.matmul(bo[:, 0:128], lhsT=hidT[:, hc, jc * 128:(jc + 1) * 128],
                                                          rhs=W2sb[:, hc, :], start=(hc == 0), stop=(hc == 1)),
                             reads=["W2sb", "hidT"], writes=[bok], inc=(hc == 1))
                    k.op(dve, lambda: nc.vector.tensor_copy(out=vcaug[g][:, jc, 0:128], in_=bo[:, 0:128]), reads=[bok],
                         writes=[f"vcaug{g}"])
    es2.close()
    k.barrier()

    if DBG is not None:
        for g in range(2):
            k.dma(sp, DBG["kc"].ap()[g], kcT[g][:], reads=[f"kcT{g}"], writes=[("dbgkc", g)])
            k.dma(sp, DBG["vc"].ap()[g], vcaug[g][:], reads=[f"vcaug{g}"], writes=[("dbgvc", g)])
        dbgst = sb("b_dbgst", [128, 512], F32)

    def dbg_dump(br, g, i):
        if DBG is None:
            return
        k.op(k.act, lambda: nc.scalar.copy(out=dbgst[:], in_=mixacc[:]), reads=[("mixacc", h_) for h_ in range(4)], writes=["dbgst"])
        k.dma(sp, DBG["o"].ap()[br, i * 128:(i + 1) * 128, g * 512:(g + 1) * 512], dbgst[:], reads=["dbgst"],
              writes=[("dbgo", br, g, i)])

    KTs = sb("b_KTs", [128, S], BF16)
    KTw = sb("b_KTw", [128, S], BF16)
    Vsa = sb("b_Vsa", [128, 32, 129], BF16)
    Vwa = sb("b_Vwa", [128, 32, 129], BF16)
    k.op(dve, lambda: nc.vector.memset(Vsa[:, :, 128:129], 1.0), writes=["Vsa"])
    k.op(dve, lambda: nc.vector.memset(Vwa[:, :, 128:129], 1.0), writes=["Vwa"])
    Qg = sb("b_Qg", [128, 16, 4, 128], BF16)
    PTs = sb("b_PTs", [128, 32, 512], BF16)
    PTw = sb("b_PTw", [128, 5, 512], BF16)
    PTc = sb("b_PTc", [128, 2, 512], BF16)
    cm = [sb(f"b_cm{i}", [128, 2, 512], BF16) for i in range(2)]
    Rsel = [sb(f"b_Rsel{i}", [69, 512], BF16) for i in range(2)]
    Rc = [sb(f"b_Rc{i}", [4, 512], BF16) for i in range(2)]
    mixacc = sb("b_mixacc", [128, 512], F32)
    mixbf = [sb(f"b_mixbf{i}", [128, 512], BF16) for i in range(2)]
    impacc = sb("b_imp", [128, 64], F32)
    score = sb("b_score", [128, 64], F32)
    work = sb("b_work", [128, 64], F32)
    t1 = sb("b_t1", [128, 64], F32)
    t2 = sb("b_t2", [128, 64], F32)
    m8 = sb("b_m8", [128, 16], F32)
    sbq = sb("b_sbq", [128, 64], BF16)
    dn = sb("b_dn", [128, 4], F32)
    dn4 = sb("b_dn4", [128, 12], F32)
    imp4 = sb("b_imp4", [128, 4, 64], F32)

    vs_reads = [("Vs", p_, t_, 0) for p_ in range(2) for t_ in range(16)]
    vw_reads = [("Vw", p_, t_, 0) for p_ in range(2) for t_ in range(16)]
    def issue_tables(g_, i_, b_):
        k.dma(sp, Rsel[b_][64:69, :], C["Rstat"].ap()[g_, i_], writes=[f"Rsel{b_}"])
        k.dma(sp, Rc[b_][:], C["Rc"].ap()[g_, i_], writes=[f"Rc{b_}"])
        k.dma(sp, cm[b_][:], C["cmpmask"].ap()[i_], writes=[f"cm{b_}"])

    it = 0
    for g in range(2):
        k.dma(sp, KTs[:], T["KTs"].ap()[g], reads=[("KTs", g, 0), ("KTs", g, 1)], writes=["KTs"])
        k.dma(sp, KTw[:], T["KTw"].ap()[g], reads=[("KTw", g, 0), ("KTw", g, 1)], writes=["KTw"])
        k.dma(sp, Vsa[:, :, 0:128], T["Vs"].ap()[:, g * 128:(g + 1) * 128].rearrange("(j kk) c -> kk j c", kk=128),
              reads=vs_reads, writes=["Vsa"])
        k.dma(sp, Vwa[:, :, 0:128], T["Vw"].ap()[:, g * 128:(g + 1) * 128].rearrange("(j kk) c -> kk j c", kk=128),
              reads=vw_reads, writes=["Vwa"])
        for hq in range(4):
            k.dma(sp, Qg[:, :, hq, :], T["QTb"].ap()[g * 4 + hq].rearrange("d (i q) -> d i q", q=128),
                  reads=[("QTb", g * 4 + hq, 1)], writes=["Qg"])
        for i in range(16):
            b = it % 2
            if it == 0:
                issue_tables(0, 0, 0)
            nxt = it + 1
            if nxt < 32:
                issue_tables(nxt // 16, nxt % 16, nxt % 2)
            it += 1
            kR, kRc, kcm, kmb = f"Rsel{b}", f"Rc{b}", f"cm{b}", f"mixbf{b}"
            qrhs = Qg[:, i, :, :]

            def coef_for(bo, bok, col, gcol):
                k.op(dve, lambda: nc.vector.tensor_scalar(out=dn[:, 0:1], in0=bo[:, col:col + 1], scalar1=1e-30,
                                                          scalar2=None, op0=ALU.max), reads=[bok], writes=["dn"])
                k.op(dve, lambda: nc.vector.reciprocal(out=dn[:, 1:2], in_=dn[:, 0:1]), reads=["dn"], writes=["dn"])
                k.op(dve, lambda: nc.vector.tensor_tensor(out=dn[:, 2:3], in0=dn[:, 1:2], in1=gsb[:, i, gcol:gcol + 1],
                                                          op=ALU.mult), reads=["dn", "gsb"], writes=["dn"])

            for jc in range(2):
                bs, bsk = bank()
                k.op(pe, lambda: nc.tensor.matmul(bs[:], lhsT=kcT[g][:, jc * 128:(jc + 1) * 128], rhs=qrhs,
                                                  start=True, stop=False), reads=[f"kcT{g}", "Qg"], writes=[bsk], inc=False)
                k.op(pe, lambda: nc.tensor.matmul(bs[:], lhsT=Lc[0:4, jc, :], rhs=Rc[b][0:4, :], start=False, stop=False),
                     reads=["Lc", kRc], writes=[bsk], inc=False)
                k.op(pe, lambda: nc.tensor.matmul(bs[:], lhsT=ident[:], rhs=cm[b][:, jc, :], start=False, stop=True),
                     reads=["ident", kcm], writes=[bsk])
                k.op(act, lambda: nc.scalar.activation(out=PTc[:, jc, :], in_=bs[:], func=AF.Exp), reads=[bsk],
                     writes=["PTc"])
            bo_o, bok_o = bank()
            bo_d, bok_d = bank()
            for hq in range(4):
                for jc in range(2):
                    k.op(pe, lambda: nc.tensor.matmul(bo_o[:, hq * 128:(hq + 1) * 128],
                                                      lhsT=PTc[:, jc, hq * 128:(hq + 1) * 128], rhs=vcaug[g][:, jc, 0:128],
                                                      start=(jc == 0), stop=(jc == 1)),
                         reads=["PTc", f"vcaug{g}"], writes=[bok_o], inc=(hq == 3 and jc == 1))
            for hq in range(4):
                for jc in range(2):
                    k.op(pe, lambda: nc.tensor.matmul(bo_d[:, hq * 65:(hq + 1) * 65],
                                                      lhsT=PTc[:, jc, hq * 128:(hq + 1) * 128], rhs=vcaug[g][:, jc, 128:193],
                                                      start=(jc == 0), stop=(jc == 1)),
                         reads=["PTc", f"vcaug{g}"], writes=[bok_d], inc=(hq == 3 and jc == 1))
            dv = bo_d[:, 0:260].rearrange("p (h c) -> p h c", h=4)
            k.op(dve, lambda: nc.vector.tensor_scalar(out=dn4[:, 0:4], in0=dv[:, :, 0], scalar1=1e-30, scalar2=None,
                                                      op0=ALU.max), reads=[bok_d], writes=["dn4a"])
            k.op(dve, lambda: nc.vector.reciprocal(out=dn4[:, 4:8], in_=dn4[:, 0:4]), reads=["dn4a"], writes=["dn4b"])
            k.op(dve, lambda: nc.vector.tensor_scalar(out=imp4[:, 1, :], in0=dv[:, 1, 1:65], scalar1=dn4[:, 5:6],
                                                      scalar2=None, op0=ALU.mult), reads=[bok_d, "dn4b"], writes=["imp_m1"])
            k.op(dve, lambda: nc.vector.tensor_scalar(out=imp4[:, 3, :], in0=dv[:, 3, 1:65], scalar1=dn4[:, 7:8],
                                                      scalar2=None, op0=ALU.mult), reads=[bok_d, "dn4b"], writes=["imp_m3"])
            k.op(dve, lambda: nc.vector.scalar_tensor_tensor(out=imp4[:, 0, :], in0=dv[:, 0, 1:65], scalar=dn4[:, 4:5],
                                                             in1=imp4[:, 1, :], op0=ALU.mult, op1=ALU.add),
                 reads=[bok_d, "dn4b", "imp_m1"], writes=["imp_s01"])
            k.op(dve, lambda: nc.vector.scalar_tensor_tensor(out=imp4[:, 2, :], in0=dv[:, 2, 1:65], scalar=dn4[:, 6:7],
                                                             in1=imp4[:, 3, :], op0=ALU.mult, op1=ALU.add),
                 reads=[bok_d, "dn4b", "imp_m3"], writes=["imp_s23"])
            k.op(dve, lambda: nc.vector.tensor_tensor(out=impacc[:], in0=imp4[:, 0, :], in1=imp4[:, 2, :], op=ALU.add),
                 reads=["imp_s01", "imp_s23"], writes=["impacc"])
            dbg_dump(0, g, i)
            k.op(dve, lambda: nc.vector.tensor_tensor(out=score[:], in0=impacc[:], in1=addtab[:, i, :], op=ALU.add),
                 reads=["impacc", "addtab"], writes=["score"])
            k.op(dve, lambda: nc.vector.max(out=m8[:, 0:8], in_=score[:]), reads=["score"], writes=["m8"])
            k.op(dve, lambda: nc.vector.match_replace(out=work[:], in_to_replace=m8[:, 0:8], in_values=score[:],
                                                      imm_value=-1e9), reads=["score", "m8"], writes=["work"])
            k.op(dve, lambda: nc.vector.max(out=m8[:, 8:16], in_=work[:]), reads=["work", "m8"], writes=["m8"])
            k.op(dve, lambda: nc.vector.tensor_scalar(out=t1[:], in0=score[:], scalar1=m8[:, 15:16], scalar2=None,
                                                      op0=ALU.is_ge), reads=["score", "m8"], writes=["t1"])
            k.op(dve, lambda: nc.vector.tensor_scalar(out=t2[:], in0=score[:], scalar1=0.0, scalar2=None,
                                                      op0=ALU.is_ge), reads=["score"], writes=["t2"])
            k.op(dve, lambda: nc.vector.tensor_tensor(out=t1[:], in0=t1[:], in1=t2[:], op=ALU.mult),
                 reads=["t1", "t2"], writes=["t1"])
            k.op(dve, lambda: nc.vector.tensor_scalar(out=sbq[:], in0=t1[:], scalar1=30000.0, scalar2=-30000.0,
                                                      op0=ALU.mult, op1=ALU.add), reads=["t1"], writes=["sbq"])
            if DBG is not None:
                k.dma(sp, DBG["score"].ap()[g, i], score[:], reads=["score"], writes=[("dbgsc", g, i)])
                k.dma(sp, DBG["t1"].ap()[g, i], t1[:], reads=["t1"], writes=[("dbgt1", g, i)])
            k.op(dve, lambda: nc.vector.tensor_tensor(out=dn4[:, 8:12], in0=dn4[:, 4:8],
                                                      in1=gsb[:, i, g * 12:g * 12 + 10:3], op=ALU.mult),
                 reads=["dn4b", "gsb"], writes=["dn4c"])
            for hq in range(4):
                k.op(dve, lambda: nc.vector.tensor_scalar(out=mixacc[:, hq * 128:(hq + 1) * 128],
                                                          in0=bo_o[:, hq * 128:(hq + 1) * 128], scalar1=dn4[:, 8 + hq:9 + hq],
                                                          scalar2=None, op0=ALU.mult),
                     reads=[bok_o, "dn4c"], writes=[("mixacc", hq)])
            bt, btk = bank()
            pv = bt[:].bitcast(BF16)
            k.op(pe, lambda: nc.tensor.transpose(out=pv[0:64, 0:128], in_=sbq[:, 0:64], identity=ident[:]),
                 reads=["sbq", "ident"], writes=[btk])
            for hq in range(4):
                k.op(dve, lambda: nc.vector.tensor_copy(out=Rsel[b][0:64, hq * 128:(hq + 1) * 128], in_=pv[0:64, 0:128]),
                     reads=[btk], writes=[(kR, hq)])
            nj = 17 + i
            for j in range(nj):
                bs, bsk = bank()
                last = (j == nj - 1)
                k.op(pe, lambda: nc.tensor.matmul(bs[:], lhsT=KTs[:, j * 128:(j + 1) * 128], rhs=qrhs, start=True,
                                                  stop=False), reads=["KTs", "Qg"], writes=[bsk], inc=False)
                k.op(pe, lambda: nc.tensor.matmul(bs[:], lhsT=Ltab[0:69, j, :], rhs=Rsel[b][0:69, :], start=False,
                                                  stop=not last), reads=["Ltab", kR] + [(kR, h_) for h_ in range(4)],
                     writes=[bsk], inc=not last)
                if last:
                    k.op(pe, lambda: nc.tensor.matmul(bs[:], lhsT=ident[:], rhs=maskLE[:], start=False, stop=True),
                         reads=["ident", "maskLE"], writes=[bsk])
                k.op(act, lambda: nc.scalar.activation(out=PTs[:, j, :], in_=bs[:], func=AF.Exp), reads=[bsk],
                     writes=[("PTs", j)])
            for hq in range(4):
                bo, bok = bank()
                for j in range(nj):
                    k.op(pe, lambda: nc.tensor.matmul(bo[:, 0:129], lhsT=PTs[:, j, hq * 128:(hq + 1) * 128],
                                                      rhs=Vsa[:, j, :], start=(j == 0), stop=(j == nj - 1)),
                         reads=[("PTs", j), "Vsa"], writes=[bok], inc=(j == nj - 1))
                coef_for(bo, bok, 128, (g * 4 + hq) * 3 + 1)
                ms = mixacc[:, hq * 128:(hq + 1) * 128]
                k.op(dve, lambda: nc.vector.scalar_tensor_tensor(out=ms, in0=bo[:, 0:128], scalar=dn[:, 2:3], in1=ms,
                                                                 op0=ALU.mult, op1=ALU.add),
                     reads=[bok, "dn", ("mixacc", hq)], writes=[("mixacc", hq)])
            dbg_dump(1, g, i)
            for jj in range(5):
                j = 12 + i + jj
                bs, bsk = bank()
                edge = jj in (0, 4)
                k.op(pe, lambda: nc.tensor.matmul(bs[:], lhsT=KTw[:, j * 128:(j + 1) * 128], rhs=qrhs, start=True,
                                                  stop=False), reads=["KTw", "Qg"], writes=[bsk], inc=False)
                k.op(pe, lambda: nc.tensor.matmul(bs[:], lhsT=Ltab[64:69, j, :], rhs=Rsel[b][64:69, :], start=False,
                                                  stop=not edge), reads=["Ltab", kR], writes=[bsk], inc=not edge)
                if edge:
                    mk, mkk = (maskGT, "maskGT") if jj == 0 else (maskLE, "maskLE")
                    k.op(pe, lambda: nc.tensor.matmul(bs[:], lhsT=ident[:], rhs=mk[:], start=False, stop=True),
                         reads=["ident", mkk], writes=[bsk])
                k.op(act, lambda: nc.scalar.activation(out=PTw[:, jj, :], in_=bs[:], func=AF.Exp), reads=[bsk],
                     writes=["PTw"])
            for hq in range(4):
                bo, bok = bank()
                for jj in range(5):
                    j = 12 + i + jj
                    k.op(pe, lambda: nc.tensor.matmul(bo[:, 0:129], lhsT=PTw[:, jj, hq * 128:(hq + 1) * 128],
                                                      rhs=Vwa[:, j, :], start=(jj == 0), stop=(jj == 4)),
                         reads=["PTw", "Vwa"], writes=[bok], inc=(jj == 4))
                coef_for(bo, bok, 128, (g * 4 + hq) * 3 + 2)
                ms = mixacc[:, hq * 128:(hq + 1) * 128]
                k.op(dve, lambda: nc.vector.scalar_tensor_tensor(out=ms, in0=bo[:, 0:128], scalar=dn[:, 2:3], in1=ms,
                                                                 op0=ALU.mult, op1=ALU.add),
                     reads=[bok, "dn", ("mixacc", hq)], writes=[("mixacc", hq)])
            dbg_dump(2, g, i)
            k.op(act, lambda: nc.scalar.copy(out=mixbf[b][:], in_=mixacc[:]), reads=[("mixacc", h_) for h_ in range(4)], writes=[kmb])
            k.dma(sp, MIX.ap()[i * 128:(i + 1) * 128, 1024 + g * 512:1024 + (g + 1) * 512], mixbf[b][:], reads=[kmb],
                  writes=[("MIXb", g, i)])
    es.close()


def phase3(k, x, W, ident, PS, MIX, out):
    nc = k.nc
    pe, act, dve, pool, sp = k.pe, k.act, k.dve, k.pool, k.sp
    es = ExitStack()

    def sb(name, shape, dt):
        return es.enter_context(nc.sbuf_tensor(name, list(shape), dt))

    mx = sb("p3_mx", [128, D], BF16)
    mn = sb("p3_mn", [128, D], BF16)
    junk = sb("p3_junk", [128, D], BF16)
    gbc = sb("p3_gbc", [128, D], F32)
    sq = sb("p3_sq", [128, 4], F32)
    tT = sb("p3_tT", [128, 16, 512], BF16)
    h1 = sb("p3_h1", [128, 4, D], F32)
    actT = sb("p3_actT", [128, 44, 512], BF16)
    wt = [sb(f"p3_wt{i}", [128, 16, 512], BF16) for i in range(3)]
    sg = [sb(f"p3_sg{i}", [128, 512], F32) for i in range(2)]
    outt = sb("p3_outt", [128, D], F32)
    st = dict(psi=0, wi=0, evi=0, sgi=0)

    def bank():
        i = st["psi"] % 8
        st["psi"] += 1
        return PS[i], f"ps{i}"

    def wtile():
        i = st["wi"] % 3
        st["wi"] += 1
        return wt[i], f"p3wt{i}"

    def transpose_to(src, skey, tl):
        for half in range(2):
            bk_t, bk = bank()
            pv = bk_t[:].bitcast(BF16)
            for j in range(8):
                kc = half * 8 + j
                k.op(pe, lambda: nc.tensor.transpose(out=pv[:, j * 128:(j + 1) * 128],
                                                     in_=src[:, kc * 128:(kc + 1) * 128], identity=ident[:]),
                     reads=[skey, "ident"], writes=[bk], inc=(j == 7))
            dst = tT[:, half * 8:half * 8 + 8, tl * 128:(tl + 1) * 128]
            srcp = pv.rearrange("p (a b) -> p a b", a=8)
            if st["evi"] % 2 == 0:
                k.op(act, lambda: nc.scalar.copy(out=dst, in_=srcp), reads=[bk], writes=[("tT", tl)])
            else:
                k.op(dve, lambda: nc.vector.tensor_copy(out=dst, in_=srcp), reads=[bk], writes=[("tT", tl)])
            st["evi"] += 1

    x_v = x.ap().rearrange("(t p) f -> t p f", p=128)
    mix_v = MIX.ap().rearrange("(t p) f -> t p f", p=128)
    out_v = out.ap().rearrange("(t p) f -> t p f", p=128)
    tTkeys = [("tT", t) for t in range(4)]

    for tb in range(4):
        k.dma(sp, gbc[:], W["gab"].ap().to_broadcast([128, D]), writes=["gbc"])
        for tl in range(4):
            tt = tb * 4 + tl
            k.dma(sp, mx[:], mix_v[tt], reads=[("MIXa", h_) for h_ in range(8)] + [("MIXb", g_, tt) for g_ in range(2)],
                  writes=["mx"])
            k.dma(sp, h1[:, tl, :], x_v[16 + tt], writes=[("h1", tl)])
            for hf in range(2):
                k.op(act, lambda: nc.scalar.activation(out=junk[:, 0:1024], in_=mx[:, hf * 1024:(hf + 1) * 1024],
                                                       func=AF.Square, scale=1.0 / 32.0, accum_out=sq[:, hf:hf + 1]),
                     reads=["mx"], writes=["junk", "sq"])
            rsqrt_eps(k, sq, "sq", 2)
            for hf in range(2):
                k.op(dve, lambda: nc.vector.scalar_tensor_tensor(out=mn[:, hf * 1024:(hf + 1) * 1024],
                                                                 in0=mx[:, hf * 1024:(hf + 1) * 1024],
                                                                 scalar=sq[:, 2 + hf:3 + hf],
                                                                 in1=gbc[:, hf * 1024:(hf + 1) * 1024],
                                                                 op0=ALU.mult, op1=ALU.mult),
                     reads=["mx", "sq", "gbc"], writes=["mn"])
            transpose_to(mn, "mn", tl)
        for cb in range(4):
            wb, wk = wtile()
            k.dma(sp, wb[:], W["WO"].ap()[cb], reads=[("WO", cb)], writes=[wk])
            for tl in range(4):
                bk_t, bk = bank()
                for kc in range(16):
                    k.op(pe, lambda: nc.tensor.matmul(bk_t[:], lhsT=tT[:, kc, tl * 128:(tl + 1) * 128], rhs=wb[:, kc, :],
                                                      start=(kc == 0), stop=(kc == 15)),
                         reads=[wk, ("tT", tl)], writes=[bk], inc=(kc == 15))
                hs = h1[:, tl, cb * 512:(cb + 1) * 512]
                k.op(dve, lambda: nc.vector.tensor_tensor(out=hs, in0=bk_t[:], in1=hs, op=ALU.add),
                     reads=[bk, ("h1", tl)], writes=[("h1", tl)])
        k.dma(sp, gbc[:], W["g2"].ap().to_broadcast([128, D]), writes=["gbc"])
        for tl in range(4):
            k.op(act, lambda: nc.scalar.activation(out=junk[:], in_=h1[:, tl, :], func=AF.Square,
                                                   scale=1.0 / math.sqrt(D), accum_out=sq[:, 0:1]),
                 reads=[("h1", tl)], writes=["junk", "sq"])
            rsqrt_eps(k, sq, "sq", 1)
            k.op(dve, lambda: nc.vector.scalar_tensor_tensor(out=mn[:], in0=h1[:, tl, :], scalar=sq[:, 1:2], in1=gbc[:],
                                                             op0=ALU.mult, op1=ALU.mult),
                 reads=[("h1", tl), "sq", "gbc"], writes=["mn"])
            transpose_to(mn, "mn", tl)
        for t in range(11):
            wgb, wgk = wtile()
            k.dma(sp, wgb[:], W["WG"].ap()[t], reads=[("WG", t)], writes=[wgk])
            wub, wuk = wtile()
            k.dma(sp, wub[:], W["WU"].ap()[t], reads=[("WU", t)], writes=[wuk])
            for hc in range(4):
                bg_t, bgk = bank()
                for kc in range(16):
                    k.op(pe, lambda: nc.tensor.matmul(bg_t[:], lhsT=wgb[:, kc, hc * 128:(hc + 1) * 128], rhs=tT[:, kc, :],
                                                      start=(kc == 0), stop=(kc == 15)),
                         reads=[wgk] + tTkeys, writes=[bgk], inc=(kc == 15))
                bu_t, buk = bank()
                for kc in range(16):
                    k.op(pe, lambda: nc.tensor.matmul(bu_t[:], lhsT=wub[:, kc, hc * 128:(hc + 1) * 128], rhs=tT[:, kc, :],
                                                      start=(kc == 0), stop=(kc == 15)),
                         reads=[wuk] + tTkeys, writes=[buk], inc=(kc == 15))
                sgb = sg[st["sgi"] % 2]
                sgk = f"sg{st['sgi'] % 2}"
                st["sgi"] += 1
                k.op(act, lambda: nc.scalar.activation(out=sgb[:], in_=bg_t[:], func=AF.Silu), reads=[bgk], writes=[sgk])
                k.op(dve, lambda: nc.vector.tensor_tensor(out=actT[:, t * 4 + hc, :], in0=bu_t[:], in1=sgb[:],
                                                          op=ALU.mult),
                     reads=[buk, sgk], writes=[("actT", t * 4 + hc)])
        for cb in range(4):
            banks = [bank() for _ in range(4)]
            for kq in range(4):
                wb, wk = wtile()
                k.dma(sp, wb[:, 0:11, :], W["WD"].ap()[cb, kq], reads=[("WD", cb, kq)], writes=[wk])
                for tl in range(4):
                    bk_t, bk = banks[tl]
                    for kc in range(11):
                        kk = kq * 11 + kc
                        k.op(pe, lambda: nc.tensor.matmul(bk_t[:], lhsT=actT[:, kk, tl * 128:(tl + 1) * 128],
                                                          rhs=wb[:, kc, :], start=(kk == 0), stop=(kk == 43)),
                             reads=[wk, ("actT", kk)], writes=[bk], inc=(kc == 10))
            for tl in range(4):
                bk_t, bk = banks[tl]
                hs = h1[:, tl, cb * 512:(cb + 1) * 512]
                k.op(dve, lambda: nc.vector.tensor_tensor(out=hs, in0=bk_t[:], in1=hs, op=ALU.add),
                     reads=[bk, ("h1", tl)], writes=[("h1", tl)])
        k.dma(sp, gbc[:], W["gf"].ap().to_broadcast([128, D]), writes=["gbc"])
        for tl in range(4):
            tt = tb * 4 + tl
            k.op(act, lambda: nc.scalar.activation(out=junk[:], in_=h1[:, tl, :], func=AF.Square,
                                                   scale=1.0 / math.sqrt(D), accum_out=sq[:, 0:1]),
                 reads=[("h1", tl)], writes=["junk", "sq"])
            rsqrt_eps(k, sq, "sq", 1)
            k.op(dve, lambda: nc.vector.scalar_tensor_tensor(out=outt[:], in0=h1[:, tl, :], scalar=sq[:, 1:2], in1=gbc[:],
                                                             op0=ALU.mult, op1=ALU.mult),
                 reads=[("h1", tl), "sq", "gbc"], writes=["outt"])
            k.dma(sp, out_v[tt], outt[:], reads=["outt"], writes=[("out", tt)])
    es.close()


def phase1(k, x, w_in, g1, ident, PS, T):
    nc = k.nc
    pe, act, dve, pool, sp = k.pe, k.act, k.dve, k.pool, k.sp
    es = ExitStack()

    def sb(name, shape, dt):
        return es.enter_context(nc.sbuf_tensor(name, list(shape), dt))

    xt = [sb(f"p1_xt{i}", [128, D], F32) for i in range(2)]
    hn = [sb(f"p1_hn{i}", [128, D], BF16) for i in range(2)]
    junk = sb("p1_junk", [128, D], BF16)
    g1bc = sb("p1_g1bc", [128, D], F32)
    ss = [sb(f"p1_ss{i}", [128, 2], F32) for i in range(2)]
    hnT = sb("p1_hnT", [128, 16, NQ], BF16)
    wt = [sb(f"p1_wt{i}", [128, 16, 512], BF16) for i in range(2)]
    ybuf = [sb(f"p1_yb{i}", [128, NQ], BF16) for i in range(2)]
    ybuf32 = [sb(f"p1_yc{i}", [128, NQ], F32) for i in range(2)]
    xbuf = [sb(f"p1_xb{i}", [128, 512], BF16) for i in range(3)]
    gbuf = [sb(f"p1_gb{i}", [128, 24], F32) for i in range(2)]

    k.dma(sp, g1bc[:], g1.ap().to_broadcast([128, D]), writes=["g1bc"])

    w_v = w_in.ap().rearrange("(kc p) c -> p kc c", p=128)
    x_v = x.ap().rearrange("(t p) f -> t p f", p=128)

    blocks = []
    for b in range(11):
        c0 = b * 512
        subs = []
        if b in (0, 1):
            subs = [("Yq", cc, 128, "QTa", b * 4 + cc // 128) for cc in range(0, 512, 128)]
        elif b in (2, 3):
            subs = [("Y", cc, 128, "KTa", (b - 2) * 4 + cc // 128) for cc in range(0, 512, 128)]
        elif b in (4, 5):
            subs = [("X", 0, 512, "Va", (b - 4) * 512)]
        elif b in (6, 7):
            subs = [("Yq", cc, 128, "QTb", (b - 6) * 4 + cc // 128) for cc in range(0, 512, 128)]
        elif b == 8:
            subs = [("Y32", cc, 128, "KVc", cc // 128) for cc in range(0, 512, 128)]
        elif b == 9:
            subs = [("Y", 0, 128, "KTs", 0), ("Y", 128, 128, "KTs", 1), ("X", 256, 256, "Vs", 0)]
        elif b == 10:
            subs = [("Y", 0, 128, "KTw", 0), ("Y", 128, 128, "KTw", 1), ("X", 256, 256, "Vw", 0)]
        blocks.append((c0, 512, subs))
    blocks.append((5632, 24, [("G", 0, 24, "Gt", 0)]))
    qonly = {0, 1, 6, 7, 11}

    wi = 0
    yi = 0
    y32i = 0
    xi = 0
    gi = 0
    psi = 0
    evi = 0
    for pss in range(2):
        tok0 = pss * NQ
        for tt in range(16):
            xb = xt[tt % 2]
            hb = hn[tt % 2]
            sq = ss[tt % 2]
            kx, kh, ks = f"xt{tt % 2}", f"hn{tt % 2}", f"ss{tt % 2}"
            k.dma(sp, xb[:], x_v[pss * 16 + tt], writes=[kx])
            k.op(act, lambda: nc.scalar.activation(out=junk[:], in_=xb[:], func=AF.Square, scale=1.0 / math.sqrt(D),
                                                   accum_out=sq[:, 0:1]),
                 reads=[kx], writes=["junk", ks])
            rsqrt_eps(k, sq, ks)
            k.op(dve, lambda: nc.vector.scalar_tensor_tensor(out=hb[:], in0=xb[:], scalar=sq[:, 1:2], in1=g1bc[:],
                                                             op0=ALU.mult, op1=ALU.mult),
                 reads=[kx, ks, "g1bc"], writes=[kh])
            for half in range(2):
                bank = PS[psi % 8]
                bk = f"ps{psi % 8}"
                psi += 1
                pv = bank[:].bitcast(BF16)
                for j in range(8):
                    kc = half * 8 + j
                    k.op(pe, lambda: nc.tensor.transpose(out=pv[:, j * 128:(j + 1) * 128],
                                                         in_=hb[:, kc * 128:(kc + 1) * 128], identity=ident[:]),
                         reads=[kh, "ident"], writes=[bk], inc=(j == 7))
                dst = hnT[:, half * 8:half * 8 + 8, tt * 128:(tt + 1) * 128]
                src = pv.rearrange("p (a b) -> p a b", a=8)
                if evi % 2 == 0:
                    k.op(act, lambda: nc.scalar.copy(out=dst, in_=src), reads=[bk], writes=[("hnT", tt)])
                else:
                    k.op(dve, lambda: nc.vector.tensor_copy(out=dst, in_=src), reads=[bk], writes=[("hnT", tt)])
                evi += 1
        for bi, (c0, ncols, subs) in enumerate(blocks):
            if pss == 0 and bi in qonly:
                continue
            wb = wt[wi % 2]
            wk = f"wt{wi % 2}"
            wi += 1
            k.dma(pool, wb[:, :, 0:ncols], w_v[:, :, c0:c0 + ncols], writes=[wk])
            for (kind, cc, n, dname, didx) in subs:
                dst_t = T[dname]
                if kind in ("Y", "Yq", "Y32"):
                    if kind == "Y32":
                        yb = ybuf32[y32i % 2]
                        yk = f"yc{y32i % 2}"
                        y32i += 1
                    else:
                        yb = ybuf[yi % 2]
                        yk = f"yb{yi % 2}"
                        yi += 1
                    for tb in range(4):
                        bank = PS[psi % 8]
                        bk = f"ps{psi % 8}"
                        psi += 1
                        for kc in range(16):
                            k.op(pe, lambda: nc.tensor.matmul(bank[:], lhsT=wb[:, kc, cc:cc + 128],
                                                              rhs=hnT[:, kc, tb * 512:(tb + 1) * 512],
                                                              start=(kc == 0), stop=(kc == 15)),
                                 reads=[wk] + [("hnT", t) for t in range(tb * 4, tb * 4 + 4)], writes=[bk],
                                 inc=(kc == 15))
                        o = yb[:, tb * 512:(tb + 1) * 512]
                        sc = QSCALE if kind == "Yq" else 1.0
                        if evi % 2 == 0:
                            k.op(act, lambda: nc.scalar.mul(out=o, in_=bank[:], mul=sc), reads=[bk], writes=[yk])
                        else:
                            k.op(dve, lambda: nc.vector.tensor_scalar(out=o, in0=bank[:], scalar1=sc, scalar2=None,
                                                                      op0=ALU.mult), reads=[bk], writes=[yk])
                        evi += 1
                    if kind == "Yq":
                        dap = dst_t.ap()[didx]
                    else:
                        dap = dst_t.ap()[didx][:, tok0:tok0 + NQ]
                    k.dma(sp, dap, yb[:], reads=[yk], writes=[(dname, didx, pss)])
                elif kind == "X":
                    for tt in range(16):
                        bank = PS[psi % 8]
                        bk = f"ps{psi % 8}"
                        psi += 1
                        for kc in range(16):
                            k.op(pe, lambda: nc.tensor.matmul(bank[:, 0:n], lhsT=hnT[:, kc, tt * 128:(tt + 1) * 128],
                                                              rhs=wb[:, kc, cc:cc + n],
                                                              start=(kc == 0), stop=(kc == 15)),
                                 reads=[wk, ("hnT", tt)], writes=[bk], inc=(kc == 15))
                        xbf = xbuf[xi % 3]
                        xk = f"xb{xi % 3}"
                        xi += 1
                        if evi % 2 == 0:
                            k.op(act, lambda: nc.scalar.copy(out=xbf[:, 0:n], in_=bank[:, 0:n]), reads=[bk], writes=[xk])
                        else:
                            k.op(dve, lambda: nc.vector.tensor_copy(out=xbf[:, 0:n], in_=bank[:, 0:n]), reads=[bk],
                                 writes=[xk])
                        evi += 1
                        r0 = tok0 + tt * 128
                        k.dma(sp, dst_t.ap()[r0:r0 + 128, didx:didx + n], xbf[:, 0:n], reads=[xk],
                              writes=[(dname, pss, tt, didx)])
                elif kind == "G":
                    for tt in range(16):
                        bank = PS[psi % 8]
                        bk = f"ps{psi % 8}"
                        psi += 1
                        for kc in range(16):
                            k.op(pe, lambda: nc.tensor.matmul(bank[:, 0:n], lhsT=hnT[:, kc, tt * 128:(tt + 1) * 128],
                                                              rhs=wb[:, kc, cc:cc + n],
                                                              start=(kc == 0), stop=(kc == 15)),
                                 reads=[wk, ("hnT", tt)], writes=[bk], inc=(kc == 15))
                        gb = gbuf[gi % 2]
                        gk = f"gb{gi % 2}"
                        gi += 1
                        k.op(act, lambda: nc.scalar.activation(out=gb[:], in_=bank[:, 0:n], func=AF.Sigmoid),
                             reads=[bk], writes=[gk])
                        k.dma(sp, dst_t.ap()[tt * 128:(tt + 1) * 128, :], gb[:], reads=[gk], writes=[(dname, tt)])
    es.close()


def alibi_slopes16():
    return np.exp2(-8.0 * np.arange(1, 17, dtype=np.float32) / 16.0).astype(np.float32)


def make_tables(half):
    sl = alibi_slopes16()
    sa = sl[0::2]
    kk = np.arange(128, dtype=np.float32)[:, None]
    qq = np.arange(128, dtype=np.float32)[None, :]
    biasA = np.zeros((8, 3, 2, 128, 256), np.float32)
    for h in range(8):
        for di, dil in enumerate((1, 4, 16)):
            relp = qq - kk + 128.0
            prev = np.where(kk >= qq, -sa[h] * dil * relp, NEG)
            reld = qq - kk
            diag = np.where(reld >= 0, -sa[h] * dil * reld, NEG)
            biasA[h, di, 0, :, 0:128] = prev
            biasA[h, di, 0, :, 128:256] = diag
            biasA[h, di, 1, :, 0:128] = prev if half == 1 else NEG
            biasA[h, di, 1, :, 128:256] = diag
    sbv = sl[1::2].reshape(2, 4)
    Ltab = np.zeros((69, 32, 128), np.float32)
    kl = np.arange(128)
    for j in range(32):
        Ltab[2 * j + kl // 64, j, kl] = 1.0
        Ltab[64, j, :] = 1.0
        Ltab[65, j, :] = 1.0
        Ltab[66, j, :] = j
        Ltab[67, j, :] = kl
        Ltab[68, j, :] = NEG if (half == 0 and j < 16) else 0.0
    Lc = np.zeros((4, 2, 128), np.float32)
    for jc in range(2):
        Lc[0, jc] = 1.0
        Lc[1, jc] = 1.0
        Lc[2, jc] = 128 * jc + kl
        Lc[3, jc] = 1.0
    kq = np.arange(128)
    le = np.where(kq[:, None] <= kq[None, :], 0.0, NEG).astype(np.float32)
    gt = np.where(kq[:, None] > kq[None, :], 0.0, NEG).astype(np.float32)
    maskLE = np.tile(le, (1, 4))
    maskGT = np.tile(gt, (1, 4))
    addtab = np.zeros((128, 16, 64), np.float32)
    for i in range(16):
        t_real = half * 2048 + 128 * i + kq
        cur = t_real // 64
        nr = np.arange(64)[None, :] - 32 * (1 - half)
        valid = (nr >= 0) & (nr <= cur[:, None])
        forced = (nr == 0) | (valid & (nr > cur[:, None] - 2))
        addtab[:, i, :] = np.where(valid, np.where(forced, 1000.0, 0.0), -1e4)
    c = (128 * np.arange(2)[None, :, None] + kl[:, None, None])
    n = np.arange(64)[None, None, :]
    ovl = (((16 * c) < (64 * n + 64)) & ((16 * c + 32) > (64 * n)) & (c <= 254)).astype(np.float32)
    Rstat = np.zeros((2, 16, 5, 4, 128), np.float32)
    Rc = np.zeros((2, 16, 4, 4, 128), np.float32)
    for g in range(2):
        for hq in range(4):
            sh = sbv[g, hq]
            for i in range(16):
                Rstat[g, i, 0, hq, :] = -sh * 128.0 * (16 + i)
                Rstat[g, i, 1, hq, :] = -sh * kq
                Rstat[g, i, 2, hq, :] = sh * 128.0
                Rstat[g, i, 3, hq, :] = sh
                Rstat[g, i, 4, hq, :] = 1.0
                Rc[g, i, 0, hq, :] = -sh * 128.0 * (16 + i)
                Rc[g, i, 1, hq, :] = -sh * kq
                Rc[g, i, 2, hq, :] = 16.0 * sh
                Rc[g, i, 3, hq, :] = 31.0 * sh
    cmpmask = np.zeros((16, 128, 2, 4, 128), np.float32)
    for i in range(16):
        t_win = 2048 + 128 * i + kq
        cc = c[:, :, 0]
        ok = (cc[:, :, None] <= 254) & ((16 * cc[:, :, None] + 31) <= t_win[None, None, :])
        if half == 0:
            ok = ok & (cc[:, :, None] >= 128)
        cmpmask[i] = np.where(ok, 0.0, NEG)[:, :, None, :]
    bf = lambda a: np.ascontiguousarray(a).astype(NBF)
    return {"biasA": biasA, "Ltab": bf(Ltab), "Lc": bf(Lc), "maskLE": bf(maskLE), "maskGT": bf(maskGT),
            "addtab": addtab, "ovl": bf(ovl), "Rstat": bf(Rstat.reshape(2, 16, 5, 512)),
            "Rc": bf(Rc.reshape(2, 16, 4, 512)), "cmpmask": bf(cmpmask.reshape(16, 128, 2, 512))}


def make_in_maps(inputs):
    xs = np.asarray(inputs["x"], dtype=np.float32)
    B = xs.shape[0]
    common = {
        "w_in": np.ascontiguousarray(np.asarray(inputs["w_in"], np.float32)[0]),
        "g1": np.ascontiguousarray(np.asarray(inputs["norm1_g"], np.float32)[0].reshape(1, D)),
        "ident": np.eye(128, dtype=np.float32).astype(NBF),
        "w_out": np.ascontiguousarray(np.asarray(inputs["w_out"], np.float32)[0]),
        "w_gate": np.ascontiguousarray(np.asarray(inputs["w_gate"], np.float32)[0]),
        "w_up": np.ascontiguousarray(np.asarray(inputs["w_up"], np.float32)[0]),
        "w_down": np.ascontiguousarray(np.asarray(inputs["w_down"], np.float32)[0]),
        "gab": np.concatenate([np.asarray(inputs["grp_norm_a"], np.float32)[0],
                               np.asarray(inputs["grp_norm_b"], np.float32)[0]]).reshape(1, D),
        "g2": np.ascontiguousarray(np.asarray(inputs["norm2_g"], np.float32)[0].reshape(1, D)),
        "gf": np.ascontiguousarray(np.asarray(inputs["final_g"], np.float32).reshape(1, D)),
        "pe_k": np.ascontiguousarray(np.asarray(inputs["cmp_pe_k"], np.float32)[0]),
        "pe_v": np.ascontiguousarray(np.asarray(inputs["cmp_pe_v"], np.float32)[0]),
        "w1_k": np.ascontiguousarray(np.asarray(inputs["cmp_w1_k"], np.float32)[0]),
        "w1_v": np.ascontiguousarray(np.asarray(inputs["cmp_w1_v"], np.float32)[0]),
        "w2_k": np.ascontiguousarray(np.asarray(inputs["cmp_w2_k"], np.float32)[0]),
        "w2_v": np.ascontiguousarray(np.asarray(inputs["cmp_w2_v"], np.float32)[0]),
    }
    maps = []
    for c in range(8):
        b, half = c // 2, c % 2
        if half == 0:
            xc = np.concatenate([np.zeros((NQ, D), np.float32), xs[b, :NQ]], axis=0)
        else:
            xc = xs[b]
        m = dict(common)
        m.update(make_tables(half))
        m["x"] = np.ascontiguousarray(xc)
        maps.append(m)
    return maps


def kernel(**inputs):
    k = build()
    maps = make_in_maps(inputs)
    res = run_bass_kernel_spmd(k.nc, maps, core_ids=list(range(8)))
    outs = [r["out"] for r in res.results]
    full = np.zeros((4, S, D), np.float32)
    for c in range(8):
        b, half = c // 2, c % 2
        full[b, half * NQ:(half + 1) * NQ] = outs[c]
    return full
```
